# Optimizing a Trainium2 kernel written in Bass

```python
import math
import jax, jax.numpy as jnp
from jax import lax
import numpy as np

D_MODEL = 1024
BATCH = 16
SEQ = 2048
DEPTH = 1

CHUNK = 64
DA_HEADS = D_MODEL // 128
DA_HEAD_DIM = 64
DA_V_DIM = 2 * DA_HEAD_DIM
DA_QK_WIDTH = DA_HEADS * 2 * DA_HEAD_DIM
DA_V_WIDTH = DA_HEADS * DA_V_DIM
ROT_DIM = DA_HEAD_DIM // 4
ROPE_THETA = 500000.0
Q_BLOCK = 128
GM_GROUPS = D_MODEL // 128
GM_GROUP_DIM = 128
GM_WIDTH = GM_GROUPS * GM_GROUP_DIM
GM_BLOCK = 128
N_BRANCHES = 2
D_FF = -(-8 * D_MODEL // (3 * 256)) * 256
EPS = 1e-6
SUBLN_EPS = 1e-5
IN_WIDTHS = (DA_QK_WIDTH, DA_QK_WIDTH, DA_V_WIDTH, GM_WIDTH, GM_WIDTH, N_BRANCHES * D_MODEL)
IN_SPLITS = tuple(int(s) for s in np.cumsum(IN_WIDTHS)[:-1])
D_IN = int(sum(IN_WIDTHS))

kernel_name = "hybrid_diffattn_gmlp_gated_block"


def rmsnorm(x, g, eps=EPS):
    xf = x.astype(jnp.float32)
    y = xf * lax.rsqrt(jnp.mean(xf * xf, axis=-1, keepdims=True) + eps)
    return (y * g.astype(jnp.float32)).astype(x.dtype)


def lambda_init_fn(layer):
    return 0.8 - 0.6 * math.exp(-0.3 * layer)


def rope_tables(positions, dtype):
    inv_freq = ROPE_THETA ** (-np.arange(0, ROT_DIM, 2, dtype=np.float32) / ROT_DIM)
    ang = positions.astype(jnp.float32)[..., None] * jnp.asarray(inv_freq)
    return jnp.cos(ang)[:, :, None, :].astype(dtype), jnp.sin(ang)[:, :, None, :].astype(dtype)


def partial_rope(x, cos, sin):
    half = ROT_DIM // 2
    x1, x2, rest = x[..., :half], x[..., half:ROT_DIM], x[..., ROT_DIM:]
    return jnp.concatenate([x1 * cos - x2 * sin, x2 * cos + x1 * sin, rest], axis=-1)


def diff_attention(q1, q2, k1, k2, v, lam):
    S = q1.shape[2]
    scale = DA_HEAD_DIM ** -0.5
    outs = []
    for qb in range(S // Q_BLOCK):
        lo, hi = qb * Q_BLOCK, (qb + 1) * Q_BLOCK
        qpos = np.arange(lo, hi)
        kpos = np.arange(hi)
        mask = (kpos[None, :] // CHUNK) <= (qpos[:, None] // CHUNK)

        def probs(q, k):
            s = jnp.einsum('bhqd,bhkd->bhqk', q[:, :, lo:hi], k[:, :, :hi]).astype(jnp.float32) * scale
            s = jnp.where(mask, s, -jnp.inf)
            return jax.nn.softmax(s, axis=-1)

        p = probs(q1, k1) - lam * probs(q2, k2)
        outs.append(jnp.einsum('bhqk,bhkd->bhqd', p.astype(v.dtype), v[:, :, :hi]))
    return jnp.concatenate(outs, axis=2)


def spatial_gating(u, vg, g_norm, w_s, b_s):
    B, S, _ = vg.shape
    vn = rmsnorm(vg, g_norm)
    vb = vn.reshape(B, S // GM_BLOCK, GM_BLOCK, GM_GROUPS, GM_GROUP_DIM)
    pos = np.arange(GM_BLOCK)
    mask = (pos[None, :] // CHUNK) <= (pos[:, None] // CHUNK)
    w = jnp.where(mask, w_s, 0.0)
    mixed = jnp.einsum('gij,bnjgc->bnigc', w, vb) + b_s.T[None, None, :, :, None]
    return u * mixed.reshape(B, S, GM_WIDTH)


def swiglu(h, w_in, w_out):
    a, b = jnp.split(h @ w_in, 2, axis=-1)
    return (jax.nn.silu(a) * b) @ w_out


def setup_inputs(seed: int = 0) -> dict:
    key = jax.random.key(seed)
    ks = jax.random.split(key, 16)
    f32 = jnp.float32
    x = jax.random.normal(ks[0], (BATCH, SEQ, D_MODEL), f32)
    offsets = jax.random.randint(ks[1], (BATCH,), 0, 4096, dtype=jnp.int32)
    positions = (jnp.arange(SEQ, dtype=jnp.int32)[None, :] + offsets[:, None]).astype(jnp.int32)
    norm_mix_g = 1.0 + 0.02 * jax.random.normal(ks[2], (DEPTH, D_MODEL), f32)
    w_in = jax.random.normal(ks[3], (DEPTH, D_MODEL, D_IN), f32) * D_MODEL ** -0.5
    gate_b = 0.02 * jax.random.normal(ks[4], (DEPTH, N_BRANCHES, D_MODEL), f32)
    lambdas = 0.1 * jax.random.normal(ks[5], (DEPTH, 4, DA_HEAD_DIM), f32)
    subln_g = 1.0 + 0.02 * jax.random.normal(ks[6], (DEPTH, DA_V_DIM), f32)
    gm_norm_g = 1.0 + 0.02 * jax.random.normal(ks[7], (DEPTH, GM_WIDTH), f32)
    gm_ws = jax.random.normal(ks[8], (DEPTH, GM_GROUPS, GM_BLOCK, GM_BLOCK), f32) * GM_BLOCK ** -0.5
    gm_bs = 1.0 + 0.02 * jax.random.normal(ks[9], (DEPTH, GM_GROUPS, GM_BLOCK), f32)
    w_out = jax.random.normal(ks[10], (DEPTH, D_MODEL, D_MODEL), f32) * D_MODEL ** -0.5
    norm_ffn_g = 1.0 + 0.02 * jax.random.normal(ks[11], (DEPTH, D_MODEL), f32)
    w_ffn_in = jax.random.normal(ks[12], (DEPTH, D_MODEL, 2 * D_FF), f32) * D_MODEL ** -0.5
    w_ffn_out = jax.random.normal(ks[13], (DEPTH, D_FF, D_MODEL), f32) * D_FF ** -0.5
    norm_final_g = 1.0 + 0.02 * jax.random.normal(ks[14], (D_MODEL,), f32)
    return {"x": x, "positions": positions, "norm_mix_g": norm_mix_g, "w_in": w_in,
            "gate_b": gate_b, "lambdas": lambdas, "subln_g": subln_g, "gm_norm_g": gm_norm_g,
            "gm_ws": gm_ws, "gm_bs": gm_bs, "w_out": w_out, "norm_ffn_g": norm_ffn_g,
            "w_ffn_in": w_ffn_in, "w_ffn_out": w_ffn_out, "norm_final_g": norm_final_g}


def reference(x, positions, norm_mix_g, w_in, gate_b, lambdas, subln_g, gm_norm_g,
              gm_ws, gm_bs, w_out, norm_ffn_g, w_ffn_in, w_ffn_out, norm_final_g):
    B, S, _ = x.shape
    cos, sin = rope_tables(positions, x.dtype)
    for l in range(DEPTH):
        lam_init = lambda_init_fn(l)
        h = rmsnorm(x, norm_mix_g[l])
        proj = h @ w_in[l]
        q, k, v, gu, gv, gates = jnp.split(proj, IN_SPLITS, axis=-1)

        q = q.reshape(B, S, DA_HEADS, 2, DA_HEAD_DIM)
        k = k.reshape(B, S, DA_HEADS, 2, DA_HEAD_DIM)
        to_bhsd = lambda t: partial_rope(t, cos, sin).transpose(0, 2, 1, 3)
        q1, q2 = to_bhsd(q[..., 0, :]), to_bhsd(q[..., 1, :])
        k1, k2 = to_bhsd(k[..., 0, :]), to_bhsd(k[..., 1, :])
        vh = v.reshape(B, S, DA_HEADS, DA_V_DIM).transpose(0, 2, 1, 3)
        lf = lambdas[l].astype(jnp.float32)
        lam = jnp.exp(jnp.sum(lf[0] * lf[1])) - jnp.exp(jnp.sum(lf[2] * lf[3])) + lam_init
        o = diff_attention(q1, q2, k1, k2, vh, lam).transpose(0, 2, 1, 3)
        o = rmsnorm(o, subln_g[l], SUBLN_EPS) * (1.0 - lam_init)
        attn_out = o.reshape(B, S, DA_V_WIDTH)

        gm_out = spatial_gating(jax.nn.gelu(gu, approximate=False), jax.nn.gelu(gv, approximate=False),
                                gm_norm_g[l], gm_ws[l], gm_bs[l])

        g = jax.nn.sigmoid(gates.reshape(B, S, N_BRANCHES, D_MODEL) + gate_b[l])
        merged = g[:, :, 0, :] * attn_out + g[:, :, 1, :] * gm_out
        x = x + merged @ w_out[l]

        x = x + swiglu(rmsnorm(x, norm_ffn_g[l]), w_ffn_in[l], w_ffn_out[l])
    return rmsnorm(x, norm_final_g)
```

```python
import math
import contextlib
import numpy as np
import concourse.bass as bass
import concourse.mybir as mybir
from concourse.bass_utils import run_bass_kernel_spmd

F32 = mybir.dt.float32
BF16 = mybir.dt.bfloat16
I32 = mybir.dt.int32
AF = mybir.ActivationFunctionType
ALU = mybir.AluOpType
AX = mybir.AxisListType

D = 1024
SEQ = 2048
NT = 16
NH = 8
DFF = 2816
NJ = 22
LAM_INIT = 0.2
EPS = 1e-6
SUBLN_EPS = 1e-5
NSEQ = 2
N_CORES = 8


class _Op:
    __slots__ = ("eng", "idx", "fn", "deps", "sig", "chan", "chan_n", "count")

    def __init__(self, eng, idx, fn, chan):
        self.eng = eng
        self.idx = idx
        self.fn = fn
        self.deps = []
        self.sig = False
        self.chan = chan
        self.chan_n = 0
        self.count = None


class Sched:
    ENGS = ("pe", "act", "dve", "pool", "sp")

    def __init__(self):
        self.ops = {e: [] for e in self.ENGS}
        self.lastw = {}
        self.readers = {}
        self.chan_cnt = {}

    def add(self, eng, fn, r=(), w=(), chan=None):
        op = _Op(eng, len(self.ops[eng]), fn, chan)
        if chan is not None:
            self.chan_cnt[chan] = self.chan_cnt.get(chan, 0) + 1
            op.chan_n = self.chan_cnt[chan]
        deps = []
        for k in r:
            lw = self.lastw.get(k)
            if lw is not None:
                deps.append(lw)
            if isinstance(k, tuple) and k[0] == "ps":
                deps.extend(o for o in self.readers.get(k, ()) if o.eng != eng)
        for k in w:
            lw = self.lastw.get(k)
            if lw is not None:
                deps.append(lw)
            deps.extend(self.readers.get(k, ()))
        seen = set()
        for d in deps:
            if d is op or id(d) in seen:
                continue
            seen.add(id(d))
            if d.chan is None and d.eng == "pe" and eng == "pe" and chan is None:
                continue
            op.deps.append(d)
            if d.chan is None:
                d.sig = True
        for k in r:
            lst = self.readers.setdefault(k, [])
            if chan is None:
                lst[:] = [o for o in lst if not (o.chan is None and o.eng == eng)]
            lst.append(op)
        for k in w:
            self.lastw[k] = op
            self.readers[k] = []
        self.ops[eng].append(op)
        return op

    def emit(self, nc):
        with contextlib.ExitStack() as es:
            esem = {e: es.enter_context(nc.semaphore("s_" + e)) for e in self.ENGS}
            csem = {}
            for i, c in enumerate(self.chan_cnt):
                csem[c] = es.enter_context(nc.semaphore("c%d" % i))
            for e in self.ENGS:
                n = 0
                for op in self.ops[e]:
                    if op.chan is None and op.sig:
                        n += 1
                        op.count = n
            ops = self.ops

            def run(e, eng):
                waited = {}
                for op in ops[e]:
                    need = {}
                    for d in op.deps:
                        if d.chan is not None:
                            s, v = csem[d.chan], 16 * d.chan_n
                        else:
                            s, v = esem[d.eng], d.count
                        key = id(s)
                        if v > need.get(key, (None, 0))[1]:
                            need[key] = (s, v)
                    for key, (s, v) in need.items():
                        if waited.get(key, 0) >= v:
                            continue
                        waited[key] = v
                        eng.wait_ge(s, v)
                    ins = op.fn(eng)
                    if op.chan is not None:
                        ins.then_inc(csem[op.chan], 16)
                    elif op.sig:
                        ins.then_inc(esem[e], 1)

            with nc.Block() as block:
                @block.tensor
                def _(eng):
                    run("pe", eng)

                @block.scalar
                def _(eng):
                    run("act", eng)

                @block.vector
                def _(eng):
                    run("dve", eng)

                @block.gpsimd
                def _(eng):
                    run("pool", eng)

                @block.sync
                def _(eng):
                    run("sp", eng)


class Rot:
    def __init__(self, ids):
        self.ids = list(ids)
        self.i = 0

    def next(self):
        b = self.ids[self.i % len(self.ids)]
        self.i += 1
        return b


class _Stop(Exception):
    pass


def build_program(stop=None):
    nc = bass.Bass("TRN2", target_bir_lowering=False)

    def ck(name):
        if stop == name:
            raise _Stop()

    def din(name, shape, dt=F32):
        return nc.dram_tensor(name, shape, dt, kind="ExternalInput").ap()

    x_d = din("x", [NSEQ, SEQ, D])
    pos_d = din("pos", [NSEQ, 128, NT], I32)
    wqkv_d = din("w_qkv", [NH, D, 384])
    wgv_d = din("w_gv", [D, D])
    wug_d = din("w_ug", [NH, D, 384])
    wout_d = din("w_out", [D, D])
    wffi_d = din("w_ffi", [NJ, D, 256])
    wffo_d = din("w_ffo", [NJ, 128, D])
    gmix_d = din("g_mix", [128, 8])
    gffn_d = din("g_ffn", [128, 8])
    ggm_d = din("g_gm", [128, 8])
    gateb_d = din("gate_bl", [128, 16])
    lam_d = din("lambdas", [1, 256])
    subg_d = din("subln_g", [1, 128])
    gmw_d = din("gm_wT", [128, 8 * 128])
    gmb_d = din("gm_b", [1, 1024])
    gfin_d = din("g_fin", [1, 1024])
    ident_d = din("ident", [128, 128])
    invf_d = din("invf", [1, 8])
    out_d = nc.dram_tensor("out", [NSEQ, SEQ, D], F32, kind="ExternalOutput").ap()
    x1_d = nc.dram_tensor("x1s", [NSEQ, SEQ, D], F32, kind="Internal").ap()

    S = Sched()
    with contextlib.ExitStack() as es:
        def sb(name, shape, dt):
            return es.enter_context(nc.sbuf_tensor("sb_" + name, shape, dt))

        def ps(name, shape, dt):
            return es.enter_context(nc.psum_tensor("ps_" + name, shape, dt))

        bufA = sb("bufA", [128, 8, SEQ], BF16)
        bufBC = sb("bufBC", [128, 32768], BF16)
        wbuf = sb("wbuf", [128, 8192], BF16)
        ws = [sb("ws%d" % i, [128, 3072], BF16) for i in range(3)]
        junk = sb("junk", [128, 1024], BF16)
        qk2 = [sb("qk_h%d" % i, [128, 2, SEQ], BF16) for i in range(2)]
        v2 = [sb("v_h%d" % i, [128, NT, 132], BF16) for i in range(2)]
        pt = [sb("pt%d" % i, [128, 2, 512], BF16) for i in range(3)]
        xn = [sb("xn%d" % i, [128, 1024], BF16) for i in range(2)]
        xn3 = xn + [sb("xn2", [128, 1024], BF16)]
        scr = sb("scr", [128, 5, 512], F32)
        b_half = sb("b_half", [128, 1024], F32)
        gfin = sb("gfin", [128, 1024], F32)
        wT = sb("wT", [128, 8, 128], BF16)
        identf = sb("identf", [128, 128], F32)
        identb = sb("identb", [128, 128], BF16)
        gmix = sb("gmix", [128, 8], F32)
        gffn = sb("gffn", [128, 8], F32)
        gnh = sb("gnh", [128, 8], F32)
        hb = sb("hb", [128, 16], F32)
        lamt = sb("lamt", [128, 256], F32)
        lp = sb("lp", [128, 2, 64], F32)
        lsm = sb("lsm", [128, 4], F32)
        nlam = sb("nlam", [128, 1], F32)
        subg = sb("subg", [128, 128], F32)
        invf = sb("invf", [128, 8], F32)
        posi = sb("posi", [128, NT], I32)
        posf = sb("posf", [128, NT], F32)
        ang = sb("ang", [128, NT, 8], F32)
        ang2 = sb("ang2", [128, NT, 8], F32)
        angi = sb("angi", [128, NT, 8], I32)
        angr = sb("angr", [128, NT, 8], F32)
        sc = sb("sc", [128, NT, 32], F32)
        qkb = [sb("qkb%d" % i, [128, 256], BF16) for i in range(3)]
        stg = sb("stg", [128, 384], F32)
        rt = sb("rt", [128, 4, 16], F32)
        ru = sb("ru", [128, 4, 16], F32)
        o_sb = sb("o_sb", [128, 4, 128], F32)
        atok = sb("atok", [128, 4, 128], BF16)
        acc_sb = sb("acc_sb", [128, 4, 2, 132], F32)
        osq = junk[:, 0:512].rearrange("p (j c) -> p j c", c=128)
        rr = sb("rr", [128, 8], F32)
        st = sb("st", [128, 8, NT], F32)
        rstd = sb("rstd", [128, 8, NT], F32)
        rq = sb("rq", [128, 2, NT], F32)
        fence = sb("fence", [128, 4], F32)
        lnt = sb("lnt", [128, NT], F32)
        epsb = sb("epsb", [128, 1], F32)

        banks = [ps("bank%d" % i, [128, 512], F32) for i in range(8)]

        hT = bufA
        vn = bufBC[:, 0:16384].rearrange("p (t c) -> p t c", c=1024)
        attnT = bufBC[:, 16384:32768].rearrange("p (h s) -> p h s", s=SEQ)
        fT = bufBC[:, 0:NJ * 1024].rearrange("p (j s) -> p j s", s=1024)
        wbufv = wbuf[:].rearrange("p (k c) -> p k c", c=1024)
        x2a = wbuf[:].bitcast(F32).rearrange("p (t c) -> p t c", c=512)
        ws384 = [w[:].rearrange("p (k c) -> p k c", c=384) for w in ws]
        wsffi = [w[:, 0:2048].rearrange("p (k c) -> p k c", c=256) for w in ws]
        wsffo = [w[:].rearrange("p (j c) -> p j c", c=512) for w in ws]

        def bankbf(b):
            return banks[b][:].bitcast(BF16)

        def vn_w(t):
            return [("vn", t), ("fT", t, 0), ("fT", t, 1)]

        def at_w(h, t):
            ks = [("at", h, t)]
            j = 16 + 2 * h + t // 8
            if j < NJ:
                ks.append(("fT", j, (t % 8) // 4))
            return ks

        def fT_w(j, tg2):
            ks = [("fT", j, tg2)]
            if j < 16:
                ks.append(("vn", j))
            else:
                h = (j - 16) // 2
                t0 = ((j - 16) % 2) * 8 + tg2 * 4
                ks += [("at", h, t0 + i) for i in range(4)]
            return ks

        X2A = [("x2a", t) for t in range(8)]
        XS_AP = []
        for i in range(4):
            flat = qk2[i // 2][:].rearrange("p a s -> p (a s)").bitcast(F32)
            XS_AP.append(flat[:, (i % 2) * 1024:(i % 2) * 1024 + 1024])

        def xs_half(i, half):
            return XS_AP[i][:, half * 512:(half + 1) * 512]

        def xs_hk(i, half):
            nm = "q" if i % 2 == 0 else "k"
            return [("xsh", i, half)] + [(nm, i // 2, t) for t in range(half * 8, half * 8 + 8)]

        XS_K = [xs_hk(i, 0) + xs_hk(i, 1) for i in range(4)]

        def qk_w(par, t):
            return [("q", par, t), ("k", par, t), ("xsh", 2 * par, t // 8), ("xsh", 2 * par + 1, t // 8)]

        def xslot(i):
            return xs_half(i // 2, i % 2)

        def xslot_keys(i):
            return xs_hk(i // 2, i % 2)

        deferred = []
        late_ops = []

        def flush_deferred():
            for f in deferred:
                f()
            del deferred[:]

        rot = Rot(range(8))
        rotA = Rot(range(3))
        wsrot = Rot(range(3))
        ptrot = Rot(range(3))
        setup_n = [0]

        def setup_load(eng, out, in_, w):
            setup_n[0] += 1
            S.add(eng, lambda e: e.dma_start(out=out, in_=in_), w=w, chan=("setup", setup_n[0]))

        def rsqrt_chain(src, dst, n, scale, eps, rkeys, wkeys, add=None):
            add = add or S.add
            v = rq[:, 0, 0:n]
            t = rq[:, 1, 0:n]
            vi = v.bitcast(I32)
            di = dst.bitcast(I32)
            add("dve", lambda e: e.tensor_scalar(out=v, in0=src, scalar1=scale, scalar2=eps,
                                                   op0=ALU.mult, op1=ALU.add), r=rkeys, w=["rq0"])
            add("dve", lambda e: e.tensor_single_scalar(out=di, in_=vi, scalar=1,
                                                          op=ALU.arith_shift_right), r=["rq0"], w=wkeys)
            add("dve", lambda e: e.tensor_scalar(out=di, in0=di, scalar1=-1, scalar2=0x5f3759df,
                                                   op0=ALU.mult, op1=ALU.add), r=wkeys, w=wkeys)
            for _ in range(2):
                add("dve", lambda e: e.tensor_tensor(out=t, in0=dst, in1=dst, op=ALU.mult), r=wkeys, w=["rq1"])
                add("dve", lambda e: e.tensor_tensor(out=t, in0=t, in1=v, op=ALU.mult), r=["rq1", "rq0"], w=["rq1"])
                add("dve", lambda e: e.tensor_scalar(out=t, in0=t, scalar1=-0.5, scalar2=1.5,
                                                       op0=ALU.mult, op1=ALU.add), r=["rq1"], w=["rq1"])
                add("dve", lambda e: e.tensor_tensor(out=dst, in0=dst, in1=t, op=ALU.mult), r=wkeys + ["rq1"], w=wkeys)

        out_keys = []
        try:
            def rsqrt_act(src, dst, n, scale, eps, rkeys, wkeys):
                lt = lnt[:, 0:n]
                S.add("act", lambda e: e.activation(out=lt, in_=src, func=AF.Ln, scale=scale, bias=epsb[:, 0:1]),
                      r=rkeys + ["epsb"], w=["lnt"])
                S.add("act", lambda e: e.activation(out=dst, in_=lt, func=AF.Exp, scale=-0.5), r=["lnt"], w=wkeys)

            for t in range(3):
                S.add("sp", lambda e, t=t: e.dma_start(out=XS_AP[t], in_=x_d[0, t * 128:(t + 1) * 128, :]),
                      w=XS_K[t], chan=("xs", t))
            setup_load("sp", identf[:], ident_d, ["identf"])
            setup_load("sp", gmix[:], gmix_d, ["gmix"])
            setup_load("sp", invf[:], invf_d.partition_broadcast(128), ["invf"])
            S.add("dve", lambda e: e.memset(epsb[:], EPS), w=["epsb"])
            S.add("dve", lambda e: e.tensor_copy(out=identb[:], in_=identf[:]), r=["identf"], w=["identb"])

            def setup_b():
                setup_load("sp", gffn[:], gffn_d, ["gffn"])
                setup_load("sp", gnh[:], ggm_d, ["gnh"])
                setup_load("sp", hb[:], gateb_d, ["hb"])
                setup_load("sp", lamt[:], lam_d.partition_broadcast(128), ["lamt"])
                setup_load("sp", subg[:], subg_d.partition_broadcast(128), ["subg"])
                setup_load("sp", b_half[:], gmb_d.partition_broadcast(128), ["b_half"])
                setup_load("sp", gfin[:], gfin_d.partition_broadcast(128), ["gfin"])
                setup_load("sp", scr[:, 0:2, :].rearrange("p a c -> p (a c)"), gmw_d, [("scr", 0), ("scr", 1)])
                S.add("dve", lambda e: e.tensor_copy(
                    out=wT[:], in_=scr[:, 0:2, :].rearrange("p a (g i) -> p (a g) i", i=128)),
                    r=[("scr", 0), ("scr", 1)], w=["wT"])
                S.add("dve", lambda e: e.memset(wT[64:128, :, 0:64], 0.0), w=["wT"])
                S.add("dve", lambda e: e.tensor_scalar(out=gnh[:], in0=gnh[:], scalar1=0.5, scalar2=None, op0=ALU.mult),
                      r=["gnh"], w=["gnh"])
                S.add("dve", lambda e: e.tensor_scalar(out=hb[:], in0=hb[:], scalar1=0.5, scalar2=None, op0=ALU.mult),
                      r=["hb"], w=["hb"])
                S.add("dve", lambda e: e.tensor_scalar(out=b_half[:], in0=b_half[:], scalar1=0.5, scalar2=None,
                                                       op0=ALU.mult), r=["b_half"], w=["b_half"])
                S.add("dve", lambda e: e.tensor_scalar(out=subg[:], in0=subg[:], scalar1=0.5 * (1.0 - LAM_INIT),
                                                       scalar2=None, op0=ALU.mult), r=["subg"], w=["subg"])
                S.add("dve", lambda e: e.memset(v2[0][:, :, 128:132], 1.0), w=["vones"])
                S.add("dve", lambda e: e.memset(v2[1][:, :, 128:132], 1.0), w=["vones"])
                S.add("dve", lambda e: e.tensor_tensor(out=lp[:, 0, :], in0=lamt[:, 0:64], in1=lamt[:, 64:128],
                                                       op=ALU.mult), r=["lamt"], w=["lp0"])
                S.add("dve", lambda e: e.tensor_tensor(out=lp[:, 1, :], in0=lamt[:, 128:192], in1=lamt[:, 192:256],
                                                       op=ALU.mult), r=["lamt"], w=["lp1"])
                S.add("dve", lambda e: e.reduce_sum(out=lsm[:, 0:2], in_=lp[:], axis=AX.X), r=["lp0", "lp1"], w=["lsm"])
                S.add("act", lambda e: e.activation(out=lsm[:, 2:4], in_=lsm[:, 0:2], func=AF.Exp), r=["lsm"], w=["lse"])
                S.add("dve", lambda e: e.tensor_tensor(out=nlam[:], in0=lsm[:, 3:4], in1=lsm[:, 2:3], op=ALU.subtract),
                      r=["lse"], w=["nlam"])
                S.add("dve", lambda e: e.tensor_scalar(out=nlam[:], in0=nlam[:], scalar1=-LAM_INIT, scalar2=None,
                                                       op0=ALU.add), r=["nlam"], w=["nlam"])

            ck("setup")
            for s in range(NSEQ):
                S.add("sp", lambda e, s=s: e.dma_start(out=posi[:], in_=pos_d[s]), w=["posi"], chan="posi")
                S.add("dve", lambda e: e.tensor_copy(out=posf[:], in_=posi[:]), r=["posi"], w=["posf"])
                S.add("dve", lambda e: e.tensor_tensor(out=ang[:], in0=posf[:].unsqueeze(2).to_broadcast([128, NT, 8]),
                                                       in1=invf[:].unsqueeze(1).to_broadcast([128, NT, 8]), op=ALU.mult),
                      r=["posf", "invf"], w=["ang"])
                S.add("dve", lambda e: e.tensor_copy(out=angi[:], in_=ang[:]), r=["ang"], w=["angi"])
                S.add("dve", lambda e: e.tensor_copy(out=angr[:], in_=angi[:]), r=["angi"], w=["angr"])
                S.add("dve", lambda e: e.tensor_tensor(out=ang2[:], in0=ang[:], in1=angr[:], op=ALU.subtract),
                      r=["ang", "angr"], w=["ang2"])
                S.add("act", lambda e: e.activation(out=sc[:, :, 24:32], in_=ang2[:], func=AF.Sin, scale=6.283185),
                      r=["ang2"], w=["sc"])
                S.add("act", lambda e: e.activation(out=sc[:, :, 16:24], in_=ang2[:], func=AF.Sin, scale=-6.283185),
                      r=["ang2"], w=["sc"])
                S.add("dve", lambda e: e.tensor_scalar(out=ang[:], in0=ang[:], scalar1=0.25, scalar2=None, op0=ALU.add),
                      r=["ang"], w=["ang"])
                S.add("dve", lambda e: e.tensor_copy(out=angi[:], in_=ang[:]), r=["ang"], w=["angi"])
                S.add("dve", lambda e: e.tensor_copy(out=angr[:], in_=angi[:]), r=["angi"], w=["angr"])
                S.add("dve", lambda e: e.tensor_tensor(out=ang2[:], in0=ang[:], in1=angr[:], op=ALU.subtract),
                      r=["ang", "angr"], w=["ang2"])
                S.add("act", lambda e: e.activation(out=sc[:, :, 0:8], in_=ang2[:], func=AF.Sin, scale=6.283185),
                      r=["ang2"], w=["sc"])
                S.add("act", lambda e: e.activation(out=sc[:, :, 8:16], in_=ang2[:], func=AF.Sin, scale=6.283185),
                      r=["ang2"], w=["sc"])

                ck("rope")
                def s0_a(t, load=True):
                    sl = t % 4
                    if load:
                        S.add("sp", lambda e, sl=sl, t=t, s=s: e.dma_start(out=XS_AP[sl], in_=x_d[s, t * 128:(t + 1) * 128, :]),
                              w=XS_K[sl], chan=("xs", sl))
                    S.add("dve", lambda e, sl=sl, t=t: e.scalar_tensor_tensor(
                        out=junk[:], in0=XS_AP[sl], scalar=1.0, in1=XS_AP[sl], op0=ALU.mult, op1=ALU.mult,
                        accum_out=st[:, 0, t:t + 1]), r=XS_K[sl], w=[("junk", i) for i in range(8)] + [("st0", t)])

                def s0_b(t):
                    sl = t % 4
                    n3 = t % 3
                    rsqrt_act(st[:, 0, t:t + 1], rstd[:, 0, t:t + 1], 1, 1.0 / D, EPS, [("st0", t)], [("rstd0", t)])
                    S.add("act", lambda e, sl=sl, t=t, n3=n3: e.activation(out=xn3[n3][:], in_=XS_AP[sl], func=AF.Copy,
                                                                           scale=rstd[:, 0, t:t + 1]),
                          r=XS_K[sl] + [("rstd0", t)], w=[("xn", n3)])
                    b = rot.next()
                    for kc in range(8):
                        S.add("pe", lambda e, b=b, kc=kc, n3=n3: e.transpose(out=bankbf(b)[:, kc * 128:(kc + 1) * 128],
                                                                            in_=xn3[n3][:, kc * 128:(kc + 1) * 128],
                                                                            identity=identb[:]),
                              r=[("xn", n3), "identb"], w=[("ps", b)])
                    S.add("dve", lambda e, b=b, t=t: e.tensor_tensor(
                        out=hT[:, :, t * 128:(t + 1) * 128],
                        in0=bankbf(b).rearrange("p (k c) -> p k c", c=128),
                        in1=gmix[:].unsqueeze(2).to_broadcast([128, 8, 128]), op=ALU.mult),
                        r=[("ps", b), "gmix"], w=[("hT", t)])

                for t in range(3):
                    s0_a(t, load=(s > 0))
                for t in range(NT):
                    if t + 3 < NT:
                        s0_a(t + 3)
                    s0_b(t)

                if s == 0:
                    setup_b()
                ck("S0")

                ck("Wgv")
                def load_qkv(h, extra_r=()):
                    slot = wsrot.next()
                    S.add("pool", lambda e, slot=slot, h=h: e.dma_start(
                        out=ws384[slot], in_=wqkv_d[h].rearrange("(k p) c -> p k c", p=128)),
                        r=list(extra_r), w=[("ws", slot)], chan=("ws", slot))
                    return slot

                qslots = {0: load_qkv(0, extra_r=[("hT", 5)])}
                S.add("pool", lambda e: e.dma_start(out=wbufv, in_=wgv_d.rearrange("(k p) c -> p k c", p=128)),
                      r=[("hT", 15)], w=["wbuf"] + X2A, chan="wbuf")
                qslots[1] = load_qkv(1)
                ck("qkvload")

                def proj_slices(h, pbanks):
                    slot = qslots[h]
                    par = h % 2
                    qk_h = qk2[par]
                    v_h = v2[par]
                    out = []

                    def mm_slice(t, b, kcs):
                        def f():
                            for kc in kcs:
                                S.add("pe", lambda e, b=b, kc=kc, t=t: e.matmul(
                                    banks[b][:, 0:384], lhsT=hT[:, kc, t * 128:(t + 1) * 128], rhs=ws384[slot][:, kc, :],
                                    start=(kc == 0), stop=(kc == 7)),
                                    r=[("hT", t), ("ws", slot)], w=[("ps", b)])
                            if kcs[-1] == 7:
                                q = t % 3
                                S.add("act", lambda e, b=b, q=q: e.copy(out=qkb[q][:], in_=banks[b][:, 0:256]),
                                      r=[("ps", b)], w=[("qkb", q)])
                                S.add("act", lambda e, b=b: e.copy(out=stg[:], in_=banks[b][:, 0:384]),
                                      r=[("ps", b)], w=["stg"])
                                S.add("pool", lambda e, t=t: e.tensor_copy(out=v_h[:, t, 0:128], in_=stg[:, 256:384]),
                                      r=["stg"], w=[("v", par, t)])
                                xv = stg[:, 0:256].rearrange("p (m d) -> p m d", d=64)
                                S.add("dve", lambda e, t=t: e.tensor_tensor(
                                    out=rt[:], in0=xv[:, :, 0:16], in1=sc[:, t, 0:16].unsqueeze(1).to_broadcast([128, 4, 16]),
                                    op=ALU.mult), r=["stg", "sc"], w=["rt"])
                                S.add("dve", lambda e, t=t: e.tensor_tensor(
                                    out=ru[:, :, 0:8], in0=xv[:, :, 8:16],
                                    in1=sc[:, t, 16:24].unsqueeze(1).to_broadcast([128, 4, 8]),
                                    op=ALU.mult), r=["stg", "sc"], w=["ru0"])
                                S.add("dve", lambda e, t=t: e.tensor_tensor(
                                    out=ru[:, :, 8:16], in0=xv[:, :, 0:8],
                                    in1=sc[:, t, 24:32].unsqueeze(1).to_broadcast([128, 4, 8]),
                                    op=ALU.mult), r=["stg", "sc"], w=["ru1"])
                                S.add("dve", lambda e, q=q: e.tensor_tensor(
                                    out=qkb[q][:].rearrange("p (m d) -> p m d", d=64)[:, :, 0:16], in0=rt[:], in1=ru[:],
                                    op=ALU.add), r=["rt", "ru0", "ru1"], w=[("qkb", q)])
                        return f

                    def tr_slice(t):
                        def f():
                            q = t % 3
                            bt = rotA.next()
                            for c in range(2):
                                S.add("pe", lambda e, bt=bt, c=c, q=q: e.transpose(
                                    out=bankbf(bt)[:, c * 128:(c + 1) * 128], in_=qkb[q][:, c * 128:(c + 1) * 128],
                                    identity=identb[:]), r=[("qkb", q), "identb"], w=[("ps", bt)])
                            S.add("act", lambda e, bt=bt: e.copy(
                                out=qk_h[:, :, t * 128:(t + 1) * 128],
                                in_=bankbf(bt)[:, 0:256].rearrange("p (a c) -> p a c", c=128)),
                                r=[("ps", bt)], w=qk_w(par, t))
                        return f

                    for t in range(NT):
                        b = pbanks[t % len(pbanks)]
                        for kcs in ((0, 1), (2, 3), (4, 5), (6, 7)):
                            out.append(mm_slice(t, b, kcs))
                        if t >= 2:
                            out.append(tr_slice(t - 2))
                    out.append(tr_slice(NT - 2))
                    out.append(tr_slice(NT - 1))
                    return out

                def drip_late(n):
                    for _ in range(n):
                        if late_ops:
                            a_, k_ = late_ops.pop(0)
                            S.add(*a_, **k_)

                for f in proj_slices(0, [3, 4, 5, 6, 7]):
                    f()
                pending = []
                for h in range(NH):
                    if h + 2 < NH:
                        qslots[h + 2] = load_qkv(h + 2)
                    par = h % 2
                    qk_h = qk2[par]
                    v_h = v2[par]
                    pending = proj_slices(h + 1, [3]) if h + 1 < NH else []
                    n_iter = 40
                    per_iter = -(-len(pending) // (n_iter - 2)) if pending else 0

                    ck("S1proj")
                    for G in range(4):
                        nkb = 4 * G + 4

                        def scores(kb, G=G, par=par, qk_h=qk_h):
                            c0 = max(0, kb - 4 * G) * 128
                            p = ptrot.next()
                            for m in range(2):
                                b = rotA.next()
                                S.add("pe", lambda e, b=b, m=m, kb=kb, c0=c0: e.matmul(
                                    banks[b][:, c0:512],
                                    lhsT=qk_h[m * 64:(m + 1) * 64, 1, kb * 128:(kb + 1) * 128],
                                    rhs=qk_h[m * 64:(m + 1) * 64, 0, G * 512 + c0:(G + 1) * 512],
                                    start=True, stop=True),
                                    r=[("k", par, kb)] + [("q", par, 4 * G + i) for i in range(c0 // 128, 4)], w=[("ps", b)])
                                S.add("act", lambda e, b=b, m=m, p=p, c0=c0: e.activation(
                                    out=pt[p][:, m, c0:512], in_=banks[b][:, c0:512], func=AF.Exp, scale=0.125),
                                    r=[("ps", b)], w=[("pt", p, m)])
                            if kb >= 4 * G:
                                S.add("pool", lambda e, p=p, c0=c0: e.memset(pt[p][64:128, :, c0:c0 + 64], 0.0),
                                      w=[("pt", p, 0), ("pt", p, 1)])
                            return p, c0

                        def pv(kb, p, c0, G=G, par=par, v_h=v_h):
                            for j in range(c0 // 128, 4):
                                qi = 4 * G + j
                                bank = 4 + j
                                for m in range(2):
                                    off = m * 132
                                    S.add("pe", lambda e, bank=bank, off=off, p=p, m=m, j=j, kb=kb, qi=qi: e.matmul(
                                        banks[bank][:, off:off + 129], lhsT=pt[p][:, m, j * 128:(j + 1) * 128],
                                        rhs=v_h[:, kb, 0:129], start=(kb == 0 and m == 0), stop=(kb == qi),
                                        skip_group_check=True),
                                        r=[("pt", p, m), ("v", par, kb), "vones"], w=[("ps", bank)])
                                if kb == qi:
                                    S.add("act", lambda e, bank=bank, j=j: e.copy(
                                        out=acc_sb[:, j, :, :],
                                        in_=banks[bank][:, 0:264].rearrange("p (a c) -> p a c", c=132)),
                                        r=[("ps", bank)], w=[("accsb", j)])

                        prev = scores(0)
                        for kb in range(1, nkb):
                            cur = scores(kb)
                            pv(kb - 1, *prev)
                            prev = cur
                            drip_late(6)
                            for _ in range(per_iter):
                                if pending:
                                    pending.pop(0)()
                            if kb == min(nkb - 1, 6):
                                drip_late(1000)
                                flush_deferred()
                        pv(nkb - 1, *prev)

                        def finalize(h=h, G=G, add=None):
                            add_late = add or S.add
                            add = S.add
                            AS = [("accsb", j) for j in range(4)]
                            add("dve", lambda e: e.reciprocal(
                                out=rr[:, 0:8].rearrange("p (j m) -> p j m", m=2).unsqueeze(3),
                                in_=acc_sb[:, :, :, 128:129]), r=AS, w=["rr"])
                            add("dve", lambda e: e.tensor_tensor(
                                out=rr[:, 0:8].rearrange("p (j m) -> p j m", m=2)[:, :, 1:2],
                                in0=rr[:, 0:8].rearrange("p (j m) -> p j m", m=2)[:, :, 1:2],
                                in1=nlam[:, 0:1].unsqueeze(2).to_broadcast([128, 4, 1]), op=ALU.mult),
                                r=["rr", "nlam"], w=["rr"])
                            for j in range(4):
                                add("dve", lambda e, j=j: e.tensor_scalar(
                                    out=o_sb[:, j, :], in0=acc_sb[:, j, 0, 0:128], scalar1=rr[:, 2 * j:2 * j + 1], scalar2=None,
                                    op0=ALU.mult), r=[("accsb", j), "rr"], w=[("o", j)])
                                add("dve", lambda e, j=j: e.scalar_tensor_tensor(
                                    out=o_sb[:, j, :], in0=acc_sb[:, j, 1, 0:128], scalar=rr[:, 2 * j + 1:2 * j + 2],
                                    in1=o_sb[:, j, :], op0=ALU.mult, op1=ALU.add),
                                    r=[("accsb", j), "rr", ("o", j)], w=[("o", j)])
                                add_late("dve", lambda e, j=j: e.scalar_tensor_tensor(
                                    out=osq[:, j, :], in0=o_sb[:, j, :], scalar=1.0, in1=o_sb[:, j, :],
                                    op0=ALU.mult, op1=ALU.mult, accum_out=st[:, 1, j:j + 1]),
                                    r=[("o", j)], w=[("junk", j), ("st1", j)])
                            rsqrt_chain(st[:, 1, 0:4], rstd[:, 1, 0:4], 4, 1.0 / 128, SUBLN_EPS,
                                        [("st1", j) for j in range(4)], ["rstd1"], add=add_late)
                            for j in range(4):
                                add_late("dve", lambda e, j=j: e.scalar_tensor_tensor(
                                    out=atok[:, j, :], in0=o_sb[:, j, :], scalar=rstd[:, 1, j:j + 1], in1=subg[:],
                                    op0=ALU.mult, op1=ALU.mult), r=[("o", j), "rstd1", "subg"], w=[("atok", j)])

                            def fin_pe(h=h, G=G):
                                bt = rotA.next()
                                for j in range(4):
                                    S.add("pe", lambda e, bt=bt, j=j: e.transpose(
                                        out=bankbf(bt)[:, j * 128:(j + 1) * 128], in_=atok[:, j, :], identity=identb[:]),
                                        r=[("atok", j), "identb"], w=[("ps", bt)])
                                wk = []
                                for j in range(4):
                                    wk += at_w(h, 4 * G + j)
                                S.add("act", lambda e, bt=bt: e.copy(
                                    out=attnT[:, h, G * 512:(G + 1) * 512], in_=bankbf(bt)[:, 0:512]),
                                    r=[("ps", bt)], w=wk)
                            deferred.append(fin_pe)
                        finalize(add=lambda *a, **k: late_ops.append((a, k)))
                        if G == 3:
                            while pending:
                                pending.pop(0)()
                    ck("S1attn")

                drip_late(1000)
                flush_deferred()
                ck("S1")
                def load_ug(g):
                    slot = wsrot.next()
                    S.add("pool", lambda e, slot=slot, g=g: e.dma_start(
                        out=ws384[slot], in_=wug_d[g].rearrange("(k p) c -> p k c", p=128)),
                        w=[("ws", slot)], chan=("ws", slot))
                    return slot

                ug_pref = [load_ug(0), load_ug(1)]
                for t in range(NT):
                    bs = (rot.next(), rot.next())
                    vb = 2 * (t % 2)
                    for half in range(2):
                        b = bs[half]
                        for kc in range(8):
                            S.add("pe", lambda e, b=b, kc=kc, t=t, half=half: e.matmul(
                                banks[b][:], lhsT=hT[:, kc, t * 128:(t + 1) * 128],
                                rhs=wbufv[:, kc, half * 512:(half + 1) * 512], start=(kc == 0), stop=(kc == 7)),
                                r=[("hT", t), "wbuf"], w=[("ps", b)])
                        S.add("act", lambda e, b=b, half=half, vb=vb: e.activation(out=scr[:, vb + half, :], in_=banks[b][:],
                                                                                   func=AF.Gelu),
                              r=[("ps", b)], w=[("scr", vb + half)])
                    vg = scr[:, vb:vb + 2, :].rearrange("p a c -> p (a c)")
                    vk = [("scr", vb), ("scr", vb + 1)]
                    S.add("dve", lambda e, t=t, vg=vg: e.scalar_tensor_tensor(
                        out=junk[:], in0=vg, scalar=1.0, in1=vg, op0=ALU.mult, op1=ALU.mult,
                        accum_out=st[:, 2, t:t + 1]), r=vk, w=[("junk", i) for i in range(8)] + [("st2", t)])
                    rsqrt_chain(st[:, 2, t:t + 1], rstd[:, 2, t:t + 1], 1, 1.0 / D, EPS, [("st2", t)], [("rstd2", t)])
                    S.add("dve", lambda e, t=t, vg=vg: e.tensor_scalar(out=vn[:, t, :], in0=vg, scalar1=rstd[:, 2, t:t + 1],
                                                                       scalar2=None, op0=ALU.mult),
                          r=vk + [("rstd2", t)], w=vn_w(t))

                ck("C1")
                S.add("pool", lambda e: e.dma_start(out=wbufv, in_=wout_d.rearrange("(k p) c -> p k c", p=128)),
                      w=["wbuf"] + X2A, chan="wbuf")

                slots = ug_pref
                for g in range(NH):
                    slot = slots[g]
                    if g + 2 < NH:
                        slots.append(load_ug(g + 2))
                    for tg in range(4):
                        bu, ba, bb, bm = rot.next(), rot.next(), rot.next(), rot.next()
                        for jx, b in enumerate((bu, ba, bb)):
                            for kc in range(8):
                                S.add("pe", lambda e, b=b, kc=kc, jx=jx, tg=tg, slot=slot: e.matmul(
                                    banks[b][:], lhsT=ws384[slot][:, kc, jx * 128:(jx + 1) * 128],
                                    rhs=hT[:, kc, tg * 512:(tg + 1) * 512], start=(kc == 0), stop=(kc == 7)),
                                    r=[("ws", slot)] + [("hT", 4 * tg + i) for i in range(4)], w=[("ps", b)])
                        for i in range(4):
                            S.add("pe", lambda e, bm=bm, i=i, tg=tg, g=g: e.matmul(
                                banks[bm][:, i * 128:(i + 1) * 128], lhsT=vn[:, 4 * tg + i, g * 128:(g + 1) * 128],
                                rhs=wT[:, g, :], start=True, stop=True),
                                r=[("vn", 4 * tg + i), "wT"], w=[("ps", bm)])
                        S.add("act", lambda e, bu=bu: e.activation(out=scr[:, 0, :], in_=banks[bu][:], func=AF.Gelu),
                              r=[("ps", bu)], w=[("scr", 0)])
                        S.add("act", lambda e, ba=ba, g=g: e.activation(out=scr[:, 1, :], in_=banks[ba][:], func=AF.Tanh,
                                                                        bias=hb[:, g:g + 1], scale=0.5),
                              r=[("ps", ba), "hb"], w=[("scr", 1)])
                        S.add("act", lambda e, bb=bb, g=g: e.activation(out=scr[:, 2, :], in_=banks[bb][:], func=AF.Tanh,
                                                                        bias=hb[:, 8 + g:9 + g], scale=0.5),
                              r=[("ps", bb), "hb"], w=[("scr", 2)])
                        S.add("dve", lambda e, bm=bm, g=g: e.scalar_tensor_tensor(
                            out=scr[:, 3, :].rearrange("p (a c) -> p a c", c=128),
                            in0=banks[bm][:].rearrange("p (a c) -> p a c", c=128), scalar=gnh[:, g:g + 1],
                            in1=b_half[:, g * 128:(g + 1) * 128].unsqueeze(1).to_broadcast([128, 4, 128]),
                            op0=ALU.mult, op1=ALU.add), r=[("ps", bm), "gnh", "b_half"], w=[("scr", 3)])
                        S.add("pool", lambda e: e.tensor_tensor(out=scr[:, 3, :], in0=scr[:, 3, :], in1=scr[:, 0, :],
                                                                op=ALU.mult), r=[("scr", 3), ("scr", 0)], w=[("scr", 3)])
                        S.add("dve", lambda e: e.scalar_tensor_tensor(out=scr[:, 3, :], in0=scr[:, 2, :], scalar=1.0,
                                                                      in1=scr[:, 3, :], op0=ALU.add, op1=ALU.mult),
                              r=[("scr", 2), ("scr", 3)], w=[("scr", 3)])
                        atk = [("at", g, 4 * tg + i) for i in range(4)]
                        atw = []
                        for i in range(4):
                            atw += at_w(g, 4 * tg + i)
                        S.add("dve", lambda e, g=g, tg=tg: e.scalar_tensor_tensor(
                            out=scr[:, 4, :], in0=scr[:, 1, :], scalar=1.0, in1=attnT[:, g, tg * 512:(tg + 1) * 512],
                            op0=ALU.add, op1=ALU.mult), r=[("scr", 1)] + atk, w=[("scr", 4)])
                        S.add("pool", lambda e, g=g, tg=tg: e.tensor_tensor(
                            out=attnT[:, g, tg * 512:(tg + 1) * 512], in0=scr[:, 3, :], in1=scr[:, 4, :], op=ALU.add),
                            r=[("scr", 3), ("scr", 4)], w=atw)

                ck("C2")
                def load_ffi(j):
                    slot = wsrot.next()
                    S.add("pool", lambda e, slot=slot, j=j: e.dma_start(
                        out=wsffi[slot], in_=wffi_d[j].rearrange("(k p) c -> p k c", p=128)),
                        w=[("ws", slot)], chan=("ws", slot))
                    return slot

                ffi_pref = [load_ffi(0), load_ffi(1)]

                def c3_ld(t):
                    sl = t % 4
                    S.add("sp", lambda e, sl=sl, t=t, s=s: e.dma_start(out=XS_AP[sl], in_=x_d[s, t * 128:(t + 1) * 128, :]),
                          w=XS_K[sl], chan=("xs", sl))

                def c3_mm(t):
                    sl = t % 4
                    bs = (rot.next(), rot.next())
                    for half in range(2):
                        b = bs[half]
                        for kc in range(8):
                            S.add("pe", lambda e, b=b, kc=kc, t=t, half=half: e.matmul(
                                banks[b][:], lhsT=attnT[:, kc, t * 128:(t + 1) * 128],
                                rhs=wbufv[:, kc, half * 512:(half + 1) * 512], start=(kc == 0), stop=(kc == 7)),
                                r=[("at", kc, t), "wbuf"], w=[("ps", b)])
                        hk = xs_hk(sl, half)
                        S.add("dve", lambda e, b=b, sl=sl, half=half: e.tensor_tensor(
                            out=xs_half(sl, half), in0=banks[b][:], in1=xs_half(sl, half), op=ALU.add),
                            r=[("ps", b)] + hk, w=hk)
                    S.add("sp", lambda e, sl=sl, t=t, s=s: e.dma_start(out=x1_d[s, t * 128:(t + 1) * 128, :], in_=XS_AP[sl]),
                          r=XS_K[sl], w=[("x1", s, t)], chan=("xst", sl))
                    n3 = t % 3
                    S.add("dve", lambda e, sl=sl, t=t: e.scalar_tensor_tensor(
                        out=junk[:], in0=XS_AP[sl], scalar=1.0, in1=XS_AP[sl], op0=ALU.mult, op1=ALU.mult,
                        accum_out=st[:, 3, t:t + 1]), r=XS_K[sl], w=[("junk", i) for i in range(8)] + [("st3", t)])
                    rsqrt_act(st[:, 3, t:t + 1], rstd[:, 3, t:t + 1], 1, 1.0 / D, EPS, [("st3", t)], [("rstd3", t)])
                    S.add("act", lambda e, sl=sl, t=t, n3=n3: e.activation(out=xn3[n3][:], in_=XS_AP[sl], func=AF.Copy,
                                                                           scale=rstd[:, 3, t:t + 1]),
                          r=XS_K[sl] + [("rstd3", t)], w=[("xn", n3)])

                def c3_tr(t):
                    n3 = t % 3
                    b = rot.next()
                    for kc in range(8):
                        S.add("pe", lambda e, b=b, kc=kc, n3=n3: e.transpose(out=bankbf(b)[:, kc * 128:(kc + 1) * 128],
                                                                            in_=xn3[n3][:, kc * 128:(kc + 1) * 128],
                                                                            identity=identb[:]),
                              r=[("xn", n3), "identb"], w=[("ps", b)])
                    S.add("dve", lambda e, b=b, t=t: e.tensor_tensor(
                        out=hT[:, :, t * 128:(t + 1) * 128],
                        in0=bankbf(b).rearrange("p (k c) -> p k c", c=128),
                        in1=gffn[:].unsqueeze(2).to_broadcast([128, 8, 128]), op=ALU.mult),
                        r=[("ps", b), "gffn"], w=[("hT", t)])

                for t in range(3):
                    c3_ld(t)
                for t in range(NT):
                    if t + 3 < NT:
                        c3_ld(t + 3)
                    c3_mm(t)
                    if t >= 2:
                        c3_tr(t - 2)
                c3_tr(NT - 2)
                c3_tr(NT - 1)

                ck("C3")
                for hf in range(2):
                    slots = list(ffi_pref)
                    for j in range(NJ):
                        slot = slots[j]
                        if j + 2 < NJ:
                            slots.append(load_ffi(j + 2))
                        for tg2 in range(2):
                            tok0 = hf * 1024 + tg2 * 512
                            ba, bb = rot.next(), rot.next()
                            for c, b in ((0, ba), (1, bb)):
                                for kc in range(8):
                                    S.add("pe", lambda e, b=b, kc=kc, c=c, tok0=tok0, slot=slot: e.matmul(
                                        banks[b][:], lhsT=wsffi[slot][:, kc, c * 128:(c + 1) * 128],
                                        rhs=hT[:, kc, tok0:tok0 + 512], start=(kc == 0), stop=(kc == 7)),
                                        r=[("ws", slot)] + [("hT", tok0 // 128 + i) for i in range(4)], w=[("ps", b)])
                            q = (2 * j + tg2) % 2
                            S.add("act", lambda e, ba=ba, q=q: e.activation(out=scr[:, q, :], in_=banks[ba][:], func=AF.Silu),
                                  r=[("ps", ba)], w=[("scr", q)])
                            S.add("dve", lambda e, bb=bb, q=q, j=j, tg2=tg2: e.tensor_tensor(
                                out=fT[:, j, tg2 * 512:(tg2 + 1) * 512], in0=scr[:, q, :], in1=banks[bb][:], op=ALU.mult),
                                r=[("scr", q), ("ps", bb)], w=fT_w(j, tg2))

                    ck("FFNin")
                    S.add("dve", lambda e: e.memset(fence[:], 0.0), w=["wbuf", "fence"])
                    for rnd in range(2):
                        for tl in range(8):
                            t = hf * 8 + tl
                            S.add("sp", lambda e, tl=tl, t=t, s=s, rnd=rnd: e.dma_start(
                                out=xslot(tl), in_=x1_d[s, t * 128:(t + 1) * 128, rnd * 512:(rnd + 1) * 512]),
                                r=[("x1", s, t)], w=xslot_keys(tl), chan=("xsl", tl))
                        for jj in range(0, NJ, 6):
                            nj = min(6, NJ - jj)
                            slot = wsrot.next()
                            S.add("pool", lambda e, slot=slot, jj=jj, nj=nj, rnd=rnd: e.dma_start(
                                out=wsffo[slot][:, 0:nj, :],
                                in_=wffo_d[jj:jj + nj].rearrange("j p c -> p j c")[:, :, rnd * 512:(rnd + 1) * 512]),
                                w=[("ws", slot)], chan=("ws", slot))
                            for j in range(jj, jj + nj):
                                for tl in range(8):
                                    S.add("pe", lambda e, tl=tl, j=j, jj=jj, slot=slot: e.matmul(
                                        banks[tl][:], lhsT=fT[:, j, tl * 128:(tl + 1) * 128], rhs=wsffo[slot][:, j - jj, :],
                                        start=(j == 0), stop=(j == NJ - 1)),
                                        r=[("fT", j, tl // 4), ("ws", slot)], w=[("ps", tl)])
                        if rnd == 1 and hf == 0:
                            ffi_pref = [load_ffi(0), load_ffi(1)]
                        for tl in range(8):
                            if rnd == 0:
                                S.add("dve", lambda e, tl=tl: e.tensor_tensor(
                                    out=x2a[:, tl, :], in0=banks[tl][:], in1=xslot(tl), op=ALU.add),
                                    r=[("ps", tl), "fence"] + xslot_keys(tl), w=[("x2a", tl)])
                            else:
                                S.add("dve", lambda e, tl=tl: e.tensor_tensor(
                                    out=xslot(tl), in0=banks[tl][:], in1=xslot(tl), op=ALU.add),
                                    r=[("ps", tl)], w=xslot_keys(tl))
                        for tl in range(8):
                            jk = [("junk", 4 * (tl % 2) + i) for i in range(4)]
                            if rnd == 0:
                                S.add("act", lambda e, tl=tl: e.activation(
                                    out=junk[:, (tl % 2) * 512:(tl % 2) * 512 + 512], in_=x2a[:, tl, :], func=AF.Square,
                                    accum_out=st[:, 4, tl:tl + 1]), r=[("x2a", tl)], w=jk + [("st4", tl)])
                            else:
                                S.add("act", lambda e, tl=tl: e.activation(
                                    out=junk[:, (tl % 2) * 512:(tl % 2) * 512 + 512], in_=xslot(tl), func=AF.Square,
                                    accum_out=st[:, 5, tl:tl + 1]), r=xslot_keys(tl), w=jk + [("st5", tl)])
                        if rnd == 1:
                            S.add("dve", lambda e: e.tensor_tensor(out=st[:, 6, 0:8], in0=st[:, 4, 0:8], in1=st[:, 5, 0:8],
                                                                   op=ALU.add),
                                  r=[("st4", tl) for tl in range(8)] + [("st5", tl) for tl in range(8)], w=["st6"])
                            rsqrt_chain(st[:, 6, 0:8], rstd[:, 6, 0:8], 8, 1.0 / D, EPS, ["st6"], ["rstd6"])
                            for tl in range(8):
                                t = hf * 8 + tl
                                S.add("dve", lambda e, tl=tl: e.scalar_tensor_tensor(
                                    out=x2a[:, tl, :], in0=x2a[:, tl, :], scalar=rstd[:, 6, tl:tl + 1], in1=gfin[:, 0:512],
                                    op0=ALU.mult, op1=ALU.mult), r=[("x2a", tl), "rstd6", "gfin"], w=[("x2a", tl)])
                                S.add("sp", lambda e, tl=tl, t=t, s=s: e.dma_start(
                                    out=out_d[s, t * 128:(t + 1) * 128, 0:512], in_=x2a[:, tl, :]),
                                    r=[("x2a", tl)], w=[("out", s, t, 0)], chan=("o1", tl))
                                S.add("dve", lambda e, tl=tl: e.scalar_tensor_tensor(
                                    out=xslot(tl), in0=xslot(tl), scalar=rstd[:, 6, tl:tl + 1], in1=gfin[:, 512:1024],
                                    op0=ALU.mult, op1=ALU.mult), r=xslot_keys(tl) + ["rstd6", "gfin"], w=xslot_keys(tl))
                                S.add("sp", lambda e, tl=tl, t=t, s=s: e.dma_start(
                                    out=out_d[s, t * 128:(t + 1) * 128, 512:1024], in_=xslot(tl)),
                                    r=xslot_keys(tl), w=[("out", s, t, 1)], chan=("o2", tl))
                                out_keys.append(("out", s, t, 0))
                                out_keys.append(("out", s, t, 1))

        except _Stop:
            pass
        if stop is not None:
            out_keys = list(S.lastw.keys())
        S.add("sp", lambda e: e.nop(), r=out_keys)
        S.emit(nc)
    return nc


_CACHE = {}


def _prep_shared(inp):
    f = np.float32
    w_in = np.asarray(inp["w_in"], f)[0]
    qw, kw, vw = w_in[:, 0:1024], w_in[:, 1024:2048], w_in[:, 2048:3072]
    uw, gvw = w_in[:, 3072:4096], w_in[:, 4096:5120]
    gaw, gbw = w_in[:, 5120:6144], w_in[:, 6144:7168]
    sl = lambda a, h: a[:, h * 128:(h + 1) * 128]
    w_qkv = np.ascontiguousarray(np.stack(
        [np.concatenate([sl(qw, h), sl(kw, h), sl(vw, h)], axis=1) for h in range(NH)]))
    w_ug = np.ascontiguousarray(np.stack(
        [np.concatenate([sl(uw, g), sl(gaw, g), sl(gbw, g)], axis=1) for g in range(NH)]))
    wfi = np.asarray(inp["w_ffn_in"], f)[0]
    w_ffi = np.ascontiguousarray(np.stack(
        [np.concatenate([wfi[:, j * 128:(j + 1) * 128], wfi[:, DFF + j * 128:DFF + (j + 1) * 128]], axis=1)
         for j in range(NJ)]))
    w_ffo = np.ascontiguousarray(np.asarray(inp["w_ffn_out"], f)[0].reshape(NJ, 128, D))
    colT = lambda v: np.ascontiguousarray(np.asarray(v, f).reshape(8, 128).T)
    gate_b = np.asarray(inp["gate_b"], f)[0]
    gate_bl = np.ascontiguousarray(gate_b.reshape(2, 8, 128).transpose(2, 0, 1).reshape(128, 16))
    gm_wT = np.ascontiguousarray(np.asarray(inp["gm_ws"], f)[0].transpose(2, 0, 1).reshape(128, 1024))
    inv_freq = (500000.0 ** (-np.arange(0, 16, 2, dtype=np.float32) / 16)).astype(np.float32)
    invf = (inv_freq.astype(np.float64) / (2 * math.pi)).astype(f).reshape(1, 8)
    return dict(
        w_qkv=w_qkv, w_gv=np.ascontiguousarray(gvw), w_ug=w_ug,
        w_out=np.ascontiguousarray(np.asarray(inp["w_out"], f)[0]),
        w_ffi=w_ffi, w_ffo=w_ffo,
        g_mix=colT(inp["norm_mix_g"][0]), g_ffn=colT(inp["norm_ffn_g"][0]), g_gm=colT(inp["gm_norm_g"][0]),
        gate_bl=gate_bl,
        lambdas=np.ascontiguousarray(np.asarray(inp["lambdas"], f)[0].reshape(1, 256)),
        subln_g=np.ascontiguousarray(np.asarray(inp["subln_g"], f)[0].reshape(1, 128)),
        gm_wT=gm_wT,
        gm_b=np.ascontiguousarray(np.asarray(inp["gm_bs"], f)[0].reshape(1, 1024)),
        g_fin=np.ascontiguousarray(np.asarray(inp["norm_final_g"], f).reshape(1, 1024)),
        ident=np.eye(128, dtype=f), invf=invf,
    )


def kernel(**inputs):
    x = np.asarray(inputs["x"], np.float32)
    pos = np.asarray(inputs["positions"], np.int32)
    shared = _prep_shared(inputs)
    if "nc" not in _CACHE:
        _CACHE["nc"] = build_program()
    nc = _CACHE["nc"]
    in_maps = []
    for c in range(N_CORES):
        m = dict(shared)
        m["x"] = np.ascontiguousarray(x[c * NSEQ:(c + 1) * NSEQ])
        m["pos"] = np.ascontiguousarray(
            pos[c * NSEQ:(c + 1) * NSEQ].reshape(NSEQ, NT, 128).transpose(0, 2, 1))
        in_maps.append(m)
    res = run_bass_kernel_spmd(nc, in_maps, core_ids=list(range(N_CORES)))
    out = np.concatenate([np.asarray(r["out"]) for r in res.results], axis=0)
    return out.astype(np.float32)
```

```python
import math
import contextlib
import numpy as np
import concourse.bass as bass
import concourse.mybir as mybir
from concourse.bass_utils import run_bass_kernel_spmd

F32 = mybir.dt.float32
BF16 = mybir.dt.bfloat16
I32 = mybir.dt.int32
AF = mybir.ActivationFunctionType
ALU = mybir.AluOpType
AX = mybir.AxisListType

D = 1024
SEQ = 2048
NT = 16
NH = 8
DFF = 2816
NJ = 22
LAM_INIT = 0.2
EPS = 1e-6
SUBLN_EPS = 1e-5
NSEQ = 2
N_CORES = 8


class _Op:
    __slots__ = ("eng", "idx", "fn", "deps", "sig", "chan", "chan_n", "count")

    def __init__(self, eng, idx, fn, chan):
        self.eng = eng
        self.idx = idx
        self.fn = fn
        self.deps = []
        self.sig = False
        self.chan = chan
        self.chan_n = 0
        self.count = None


class Sched:
    ENGS = ("pe", "act", "dve", "pool", "sp")

    def __init__(self):
        self.ops = {e: [] for e in self.ENGS}
        self.lastw = {}
        self.readers = {}
        self.chan_cnt = {}

    def add(self, eng, fn, r=(), w=(), chan=None):
        op = _Op(eng, len(self.ops[eng]), fn, chan)
        if chan is not None:
            self.chan_cnt[chan] = self.chan_cnt.get(chan, 0) + 1
            op.chan_n = self.chan_cnt[chan]
        deps = []
        for k in r:
            lw = self.lastw.get(k)
            if lw is not None:
                deps.append(lw)
            if isinstance(k, tuple) and k[0] == "ps":
                deps.extend(o for o in self.readers.get(k, ()) if o.eng != eng)
        for k in w:
            lw = self.lastw.get(k)
            if lw is not None:
                deps.append(lw)
            deps.extend(self.readers.get(k, ()))
        seen = set()
        for d in deps:
            if d is op or id(d) in seen:
                continue
            seen.add(id(d))
            if d.chan is None and d.eng == "pe" and eng == "pe" and chan is None:
                continue
            op.deps.append(d)
            if d.chan is None:
                d.sig = True
        for k in r:
            lst = self.readers.setdefault(k, [])
            if chan is None:
                lst[:] = [o for o in lst if not (o.chan is None and o.eng == eng)]
            lst.append(op)
        for k in w:
            self.lastw[k] = op
            self.readers[k] = []
        self.ops[eng].append(op)
        return op

    def emit(self, nc):
        with contextlib.ExitStack() as es:
            esem = {e: es.enter_context(nc.semaphore("s_" + e)) for e in self.ENGS}
            csem = {}
            for i, c in enumerate(self.chan_cnt):
                csem[c] = es.enter_context(nc.semaphore("c%d" % i))
            for e in self.ENGS:
                n = 0
                for op in self.ops[e]:
                    if op.chan is None and op.sig:
                        n += 1
                        op.count = n
            ops = self.ops

            def run(e, eng):
                waited = {}
                for op in ops[e]:
                    need = {}
                    for d in op.deps:
                        if d.chan is not None:
                            s, v = csem[d.chan], 16 * d.chan_n
                        else:
                            s, v = esem[d.eng], d.count
                        key = id(s)
                        if v > need.get(key, (None, 0))[1]:
                            need[key] = (s, v)
                    for key, (s, v) in need.items():
                        if waited.get(key, 0) >= v:
                            continue
                        waited[key] = v
                        eng.wait_ge(s, v)
                    ins = op.fn(eng)
                    if op.chan is not None:
                        ins.then_inc(csem[op.chan], 16)
                    elif op.sig:
                        ins.then_inc(esem[e], 1)

            with nc.Block() as block:
                @block.tensor
                def _(eng):
                    run("pe", eng)

                @block.scalar
                def _(eng):
                    run("act", eng)

                @block.vector
                def _(eng):
                    run("dve", eng)

                @block.gpsimd
                def _(eng):
                    run("pool", eng)

                @block.sync
                def _(eng):
                    run("sp", eng)


class Rot:
    def __init__(self, ids):
        self.ids = list(ids)
        self.i = 0

    def next(self):
        b = self.ids[self.i % len(self.ids)]
        self.i += 1
        return b


class _Stop(Exception):
    pass


def build_program(stop=None):
    nc = bass.Bass("TRN2", target_bir_lowering=False)

    def ck(name):
        if stop == name:
            raise _Stop()

    def din(name, shape, dt=F32):
        return nc.dram_tensor(name, shape, dt, kind="ExternalInput").ap()

    x_d = din("x", [NSEQ, SEQ, D])
    pos_d = din("pos", [NSEQ, 128, NT], I32)
    wqkv_d = din("w_qkv", [NH, D, 384])
    wgv_d = din("w_gv", [D, D])
    wug_d = din("w_ug", [NH, D, 384])
    wout_d = din("w_out", [D, D])
    wffi_d = din("w_ffi", [NJ, D, 256])
    wffo_d = din("w_ffo", [NJ, 128, D])
    gmix_d = din("g_mix", [128, 8])
    gffn_d = din("g_ffn", [128, 8])
    ggm_d = din("g_gm", [128, 8])
    gateb_d = din("gate_bl", [128, 16])
    lam_d = din("lambdas", [1, 256])
    subg_d = din("subln_g", [1, 128])
    gmw_d = din("gm_wT", [128, 8 * 128])
    gmb_d = din("gm_b", [1, 1024])
    gfin_d = din("g_fin", [1, 1024])
    ident_d = din("ident", [128, 128])
    invf_d = din("invf", [1, 8])
    out_d = nc.dram_tensor("out", [NSEQ, SEQ, D], F32, kind="ExternalOutput").ap()
    x1_d = nc.dram_tensor("x1s", [NSEQ, SEQ, D], F32, kind="Internal").ap()

    S = Sched()
    with contextlib.ExitStack() as es:
        def sb(name, shape, dt):
            return es.enter_context(nc.sbuf_tensor("sb_" + name, shape, dt))

        def ps(name, shape, dt):
            return es.enter_context(nc.psum_tensor("ps_" + name, shape, dt))

        bufA = sb("bufA", [128, 8, SEQ], BF16)
        bufBC = sb("bufBC", [128, 32768], BF16)
        wbuf = sb("wbuf", [128, 8192], BF16)
        ws = [sb("ws%d" % i, [128, 3072], BF16) for i in range(3)]
        junk = sb("junk", [128, 1024], BF16)
        qk2 = [sb("qk_h%d" % i, [128, 2, SEQ], BF16) for i in range(2)]
        v2 = [sb("v_h%d" % i, [128, NT, 132], BF16) for i in range(2)]
        pt = [sb("pt%d" % i, [128, 2, 512], BF16) for i in range(3)]
        xn = [sb("xn%d" % i, [128, 1024], BF16) for i in range(2)]
        xn3 = xn + [sb("xn2", [128, 1024], BF16)]
        scr = sb("scr", [128, 5, 512], F32)
        b_half = sb("b_half", [128, 1024], F32)
        gfin = sb("gfin", [128, 1024], F32)
        wT = sb("wT", [128, 8, 128], BF16)
        identf = sb("identf", [128, 128], F32)
        identb = sb("identb", [128, 128], BF16)
        gmix = sb("gmix", [128, 8], F32)
        gffn = sb("gffn", [128, 8], F32)
        gnh = sb("gnh", [128, 8], F32)
        hb = sb("hb", [128, 16], F32)
        lamt = sb("lamt", [128, 256], F32)
        lp = sb("lp", [128, 2, 64], F32)
        lsm = sb("lsm", [128, 4], F32)
        nlam = sb("nlam", [128, 1], F32)
        subg = sb("subg", [128, 128], F32)
        invf = sb("invf", [128, 8], F32)
        posi = sb("posi", [128, NT], I32)
        posf = sb("posf", [128, NT], F32)
        ang = sb("ang", [128, NT, 8], F32)
        ang2 = sb("ang2", [128, NT, 8], F32)
        angi = sb("angi", [128, NT, 8], I32)
        angr = sb("angr", [128, NT, 8], F32)
        sc = sb("sc", [128, NT, 32], F32)
        qkb = [sb("qkb%d" % i, [128, 256], BF16) for i in range(3)]
        stg = sb("stg", [128, 384], F32)
        rt = sb("rt", [128, 4, 16], F32)
        ru = sb("ru", [128, 4, 16], F32)
        o_sb = sb("o_sb", [128, 4, 128], F32)
        atok = sb("atok", [128, 4, 128], BF16)
        acc_sb = sb("acc_sb", [128, 4, 2, 132], F32)
        osq = junk[:, 0:512].rearrange("p (j c) -> p j c", c=128)
        rr = sb("rr", [128, 8], F32)
        st = sb("st", [128, 8, NT], F32)
        rstd = sb("rstd", [128, 8, NT], F32)
        rq = sb("rq", [128, 2, NT], F32)
        fence = sb("fence", [128, 4], F32)
        lnt = sb("lnt", [128, NT], F32)
        epsb = sb("epsb", [128, 1], F32)

        banks = [ps("bank%d" % i, [128, 512], F32) for i in range(8)]

        hT = bufA
        vn = bufBC[:, 0:16384].rearrange("p (t c) -> p t c", c=1024)
        attnT = bufBC[:, 16384:32768].rearrange("p (h s) -> p h s", s=SEQ)
        fT = bufBC[:, 0:NJ * 1024].rearrange("p (j s) -> p j s", s=1024)
        wbufv = wbuf[:].rearrange("p (k c) -> p k c", c=1024)
        x2a = wbuf[:].bitcast(F32).rearrange("p (t c) -> p t c", c=512)
        ws384 = [w[:].rearrange("p (k c) -> p k c", c=384) for w in ws]
        wsffi = [w[:, 0:2048].rearrange("p (k c) -> p k c", c=256) for w in ws]
        wsffo = [w[:].rearrange("p (j c) -> p j c", c=512) for w in ws]

        def bankbf(b):
            return banks[b][:].bitcast(BF16)

        def vn_w(t):
            return [("vn", t), ("fT", t, 0), ("fT", t, 1)]

        def at_w(h, t):
            ks = [("at", h, t)]
            j = 16 + 2 * h + t // 8
            if j < NJ:
                ks.append(("fT", j, (t % 8) // 4))
            return ks

        def fT_w(j, tg2):
            ks = [("fT", j, tg2)]
            if j < 16:
                ks.append(("vn", j))
            else:
                h = (j - 16) // 2
                t0 = ((j - 16) % 2) * 8 + tg2 * 4
                ks += [("at", h, t0 + i) for i in range(4)]
            return ks

        X2A = [("x2a", t) for t in range(8)]
        XS_AP = []
        for i in range(4):
            flat = qk2[i // 2][:].rearrange("p a s -> p (a s)").bitcast(F32)
            XS_AP.append(flat[:, (i % 2) * 1024:(i % 2) * 1024 + 1024])

        def xs_half(i, half):
            return XS_AP[i][:, half * 512:(half + 1) * 512]

        def xs_hk(i, half):
            nm = "q" if i % 2 == 0 else "k"
            return [("xsh", i, half)] + [(nm, i // 2, t) for t in range(half * 8, half * 8 + 8)]

        XS_K = [xs_hk(i, 0) + xs_hk(i, 1) for i in range(4)]

        def qk_w(par, t):
            return [("q", par, t), ("k", par, t), ("xsh", 2 * par, t // 8), ("xsh", 2 * par + 1, t // 8)]

        def xslot(i):
            return xs_half(i // 2, i % 2)

        def xslot_keys(i):
            return xs_hk(i // 2, i % 2)

        deferred = []
        late_ops = []

        def flush_deferred():
            for f in deferred:
                f()
            del deferred[:]

        rot = Rot(range(8))
        rotA = Rot(range(3))
        wsrot = Rot(range(3))
        ptrot = Rot(range(3))
        setup_n = [0]

        def setup_load(eng, out, in_, w):
            setup_n[0] += 1
            S.add(eng, lambda e: e.dma_start(out=out, in_=in_), w=w, chan=("setup", setup_n[0]))

        def rsqrt_chain(src, dst, n, scale, eps, rkeys, wkeys, add=None):
            add = add or S.add
            v = rq[:, 0, 0:n]
            t = rq[:, 1, 0:n]
            vi = v.bitcast(I32)
            di = dst.bitcast(I32)
            add("dve", lambda e: e.tensor_scalar(out=v, in0=src, scalar1=scale, scalar2=eps,
                                                   op0=ALU.mult, op1=ALU.add), r=rkeys, w=["rq0"])
            add("dve", lambda e: e.tensor_single_scalar(out=di, in_=vi, scalar=1,
                                                          op=ALU.arith_shift_right), r=["rq0"], w=wkeys)
            add("dve", lambda e: e.tensor_scalar(out=di, in0=di, scalar1=-1, scalar2=0x5f3759df,
                                                   op0=ALU.mult, op1=ALU.add), r=wkeys, w=wkeys)
            for _ in range(2):
                add("dve", lambda e: e.tensor_tensor(out=t, in0=dst, in1=dst, op=ALU.mult), r=wkeys, w=["rq1"])
                add("dve", lambda e: e.tensor_tensor(out=t, in0=t, in1=v, op=ALU.mult), r=["rq1", "rq0"], w=["rq1"])
                add("dve", lambda e: e.tensor_scalar(out=t, in0=t, scalar1=-0.5, scalar2=1.5,
                                                       op0=ALU.mult, op1=ALU.add), r=["rq1"], w=["rq1"])
                add("dve", lambda e: e.tensor_tensor(out=dst, in0=dst, in1=t, op=ALU.mult), r=wkeys + ["rq1"], w=wkeys)

        out_keys = []
        try:
            def rsqrt_act(src, dst, n, scale, eps, rkeys, wkeys):
                lt = lnt[:, 0:n]
                S.add("act", lambda e: e.activation(out=lt, in_=src, func=AF.Ln, scale=scale, bias=epsb[:, 0:1]),
                      r=rkeys + ["epsb"], w=["lnt"])
                S.add("act", lambda e: e.activation(out=dst, in_=lt, func=AF.Exp, scale=-0.5), r=["lnt"], w=wkeys)

            for t in range(3):
                S.add("sp", lambda e, t=t: e.dma_start(out=XS_AP[t], in_=x_d[0, t * 128:(t + 1) * 128, :]),
                      w=XS_K[t], chan=("xs", t))
            setup_load("sp", identf[:], ident_d, ["identf"])
            setup_load("sp", gmix[:], gmix_d, ["gmix"])
            setup_load("sp", invf[:], invf_d.partition_broadcast(128), ["invf"])
            S.add("dve", lambda e: e.memset(epsb[:], EPS), w=["epsb"])
            S.add("dve", lambda e: e.tensor_copy(out=identb[:], in_=identf[:]), r=["identf"], w=["identb"])

            def setup_b():
                setup_load("sp", gffn[:], gffn_d, ["gffn"])
                setup_load("sp", gnh[:], ggm_d, ["gnh"])
                setup_load("sp", hb[:], gateb_d, ["hb"])
                setup_load("sp", lamt[:], lam_d.partition_broadcast(128), ["lamt"])
                setup_load("sp", subg[:], subg_d.partition_broadcast(128), ["subg"])
                setup_load("sp", b_half[:], gmb_d.partition_broadcast(128), ["b_half"])
                setup_load("sp", gfin[:], gfin_d.partition_broadcast(128), ["gfin"])
                setup_load("sp", scr[:, 0:2, :].rearrange("p a c -> p (a c)"), gmw_d, [("scr", 0), ("scr", 1)])
                S.add("dve", lambda e: e.tensor_copy(
                    out=wT[:], in_=scr[:, 0:2, :].rearrange("p a (g i) -> p (a g) i", i=128)),
                    r=[("scr", 0), ("scr", 1)], w=["wT"])
                S.add("dve", lambda e: e.memset(wT[64:128, :, 0:64], 0.0), w=["wT"])
                S.add("dve", lambda e: e.tensor_scalar(out=gnh[:], in0=gnh[:], scalar1=0.5, scalar2=None, op0=ALU.mult),
                      r=["gnh"], w=["gnh"])
                S.add("dve", lambda e: e.tensor_scalar(out=hb[:], in0=hb[:], scalar1=0.5, scalar2=None, op0=ALU.mult),
                      r=["hb"], w=["hb"])
                S.add("dve", lambda e: e.tensor_scalar(out=b_half[:], in0=b_half[:], scalar1=0.5, scalar2=None,
                                                       op0=ALU.mult), r=["b_half"], w=["b_half"])
                S.add("dve", lambda e: e.tensor_scalar(out=subg[:], in0=subg[:], scalar1=0.5 * (1.0 - LAM_INIT),
                                                       scalar2=None, op0=ALU.mult), r=["subg"], w=["subg"])
                S.add("dve", lambda e: e.memset(v2[0][:, :, 128:132], 1.0), w=["vones"])
                S.add("dve", lambda e: e.memset(v2[1][:, :, 128:132], 1.0), w=["vones"])
                S.add("dve", lambda e: e.tensor_tensor(out=lp[:, 0, :], in0=lamt[:, 0:64], in1=lamt[:, 64:128],
                                                       op=ALU.mult), r=["lamt"], w=["lp0"])
                S.add("dve", lambda e: e.tensor_tensor(out=lp[:, 1, :], in0=lamt[:, 128:192], in1=lamt[:, 192:256],
                                                       op=ALU.mult), r=["lamt"], w=["lp1"])
                S.add("dve", lambda e: e.reduce_sum(out=lsm[:, 0:2], in_=lp[:], axis=AX.X), r=["lp0", "lp1"], w=["lsm"])
                S.add("act", lambda e: e.activation(out=lsm[:, 2:4], in_=lsm[:, 0:2], func=AF.Exp), r=["lsm"], w=["lse"])
                S.add("dve", lambda e: e.tensor_tensor(out=nlam[:], in0=lsm[:, 3:4], in1=lsm[:, 2:3], op=ALU.subtract),
                      r=["lse"], w=["nlam"])
                S.add("dve", lambda e: e.tensor_scalar(out=nlam[:], in0=nlam[:], scalar1=-LAM_INIT, scalar2=None,
                                                       op0=ALU.add), r=["nlam"], w=["nlam"])

            ck("setup")
            for s in range(NSEQ):
                S.add("sp", lambda e, s=s: e.dma_start(out=posi[:], in_=pos_d[s]), w=["posi"], chan="posi")
                S.add("dve", lambda e: e.tensor_copy(out=posf[:], in_=posi[:]), r=["posi"], w=["posf"])
                S.add("dve", lambda e: e.tensor_tensor(out=ang[:], in0=posf[:].unsqueeze(2).to_broadcast([128, NT, 8]),
                                                       in1=invf[:].unsqueeze(1).to_broadcast([128, NT, 8]), op=ALU.mult),
                      r=["posf", "invf"], w=["ang"])
                S.add("dve", lambda e: e.tensor_copy(out=angi[:], in_=ang[:]), r=["ang"], w=["angi"])
                S.add("dve", lambda e: e.tensor_copy(out=angr[:], in_=angi[:]), r=["angi"], w=["angr"])
                S.add("dve", lambda e: e.tensor_tensor(out=ang2[:], in0=ang[:], in1=angr[:], op=ALU.subtract),
                      r=["ang", "angr"], w=["ang2"])
                S.add("act", lambda e: e.activation(out=sc[:, :, 24:32], in_=ang2[:], func=AF.Sin, scale=6.283185),
                      r=["ang2"], w=["sc"])
                S.add("act", lambda e: e.activation(out=sc[:, :, 16:24], in_=ang2[:], func=AF.Sin, scale=-6.283185),
                      r=["ang2"], w=["sc"])
                S.add("dve", lambda e: e.tensor_scalar(out=ang[:], in0=ang[:], scalar1=0.25, scalar2=None, op0=ALU.add),
                      r=["ang"], w=["ang"])
                S.add("dve", lambda e: e.tensor_copy(out=angi[:], in_=ang[:]), r=["ang"], w=["angi"])
                S.add("dve", lambda e: e.tensor_copy(out=angr[:], in_=angi[:]), r=["angi"], w=["angr"])
                S.add("dve", lambda e: e.tensor_tensor(out=ang2[:], in0=ang[:], in1=angr[:], op=ALU.subtract),
                      r=["ang", "angr"], w=["ang2"])
                S.add("act", lambda e: e.activation(out=sc[:, :, 0:8], in_=ang2[:], func=AF.Sin, scale=6.283185),
                      r=["ang2"], w=["sc"])
                S.add("act", lambda e: e.activation(out=sc[:, :, 8:16], in_=ang2[:], func=AF.Sin, scale=6.283185),
                      r=["ang2"], w=["sc"])

                ck("rope")
                def s0_a(t, load=True):
                    sl = t % 4
                    if load:
                        S.add("sp", lambda e, sl=sl, t=t, s=s: e.dma_start(out=XS_AP[sl], in_=x_d[s, t * 128:(t + 1) * 128, :]),
                              w=XS_K[sl], chan=("xs", sl))
                    S.add("dve", lambda e, sl=sl, t=t: e.scalar_tensor_tensor(
                        out=junk[:], in0=XS_AP[sl], scalar=1.0, in1=XS_AP[sl], op0=ALU.mult, op1=ALU.mult,
                        accum_out=st[:, 0, t:t + 1]), r=XS_K[sl], w=[("junk", i) for i in range(8)] + [("st0", t)])

                def s0_b(t):
                    sl = t % 4
                    n3 = t % 3
                    rsqrt_act(st[:, 0, t:t + 1], rstd[:, 0, t:t + 1], 1, 1.0 / D, EPS, [("st0", t)], [("rstd0", t)])
                    S.add("act", lambda e, sl=sl, t=t, n3=n3: e.activation(out=xn3[n3][:], in_=XS_AP[sl], func=AF.Copy,
                                                                           scale=rstd[:, 0, t:t + 1]),
                          r=XS_K[sl] + [("rstd0", t)], w=[("xn", n3)])
                    b = rot.next()
                    for kc in range(8):
                        S.add("pe", lambda e, b=b, kc=kc, n3=n3: e.transpose(out=bankbf(b)[:, kc * 128:(kc + 1) * 128],
                                                                            in_=xn3[n3][:, kc * 128:(kc + 1) * 128],
                                                                            identity=identb[:]),
                              r=[("xn", n3), "identb"], w=[("ps", b)])
                    S.add("dve", lambda e, b=b, t=t: e.tensor_tensor(
                        out=hT[:, :, t * 128:(t + 1) * 128],
                        in0=bankbf(b).rearrange("p (k c) -> p k c", c=128),
                        in1=gmix[:].unsqueeze(2).to_broadcast([128, 8, 128]), op=ALU.mult),
                        r=[("ps", b), "gmix"], w=[("hT", t)])

                for t in range(3):
                    s0_a(t, load=(s > 0))
                for t in range(NT):
                    if t + 3 < NT:
                        s0_a(t + 3)
                    s0_b(t)

                if s == 0:
                    setup_b()
                ck("S0")

                ck("Wgv")
                def load_qkv(h, extra_r=()):
                    slot = wsrot.next()
                    S.add("pool", lambda e, slot=slot, h=h: e.dma_start(
                        out=ws384[slot], in_=wqkv_d[h].rearrange("(k p) c -> p k c", p=128)),
                        r=list(extra_r), w=[("ws", slot)], chan=("ws", slot))
                    return slot

                qslots = {0: load_qkv(0, extra_r=[("hT", 5)])}
                S.add("pool", lambda e: e.dma_start(out=wbufv, in_=wgv_d.rearrange("(k p) c -> p k c", p=128)),
                      r=[("hT", 15)], w=["wbuf"] + X2A, chan="wbuf")
                qslots[1] = load_qkv(1)
                ck("qkvload")

                def proj_slices(h, pbanks):
                    slot = qslots[h]
                    par = h % 2
                    qk_h = qk2[par]
                    v_h = v2[par]
                    out = []

                    def mm_slice(t, b, kcs):
                        def f():
                            for kc in kcs:
                                S.add("pe", lambda e, b=b, kc=kc, t=t: e.matmul(
                                    banks[b][:, 0:384], lhsT=hT[:, kc, t * 128:(t + 1) * 128], rhs=ws384[slot][:, kc, :],
                                    start=(kc == 0), stop=(kc == 7)),
                                    r=[("hT", t), ("ws", slot)], w=[("ps", b)])
                            if kcs[-1] == 7:
                                q = t % 3
                                S.add("act", lambda e, b=b, q=q: e.copy(out=qkb[q][:], in_=banks[b][:, 0:256]),
                                      r=[("ps", b)], w=[("qkb", q)])
                                S.add("act", lambda e, b=b: e.copy(out=stg[:], in_=banks[b][:, 0:384]),
                                      r=[("ps", b)], w=["stg"])
                                S.add("pool", lambda e, t=t: e.tensor_copy(out=v_h[:, t, 0:128], in_=stg[:, 256:384]),
                                      r=["stg"], w=[("v", par, t)])
                                xv = stg[:, 0:256].rearrange("p (m d) -> p m d", d=64)
                                S.add("dve", lambda e, t=t: e.tensor_tensor(
                                    out=rt[:], in0=xv[:, :, 0:16], in1=sc[:, t, 0:16].unsqueeze(1).to_broadcast([128, 4, 16]),
                                    op=ALU.mult), r=["stg", "sc"], w=["rt"])
                                S.add("dve", lambda e, t=t: e.tensor_tensor(
                                    out=ru[:, :, 0:8], in0=xv[:, :, 8:16],
                                    in1=sc[:, t, 16:24].unsqueeze(1).to_broadcast([128, 4, 8]),
                                    op=ALU.mult), r=["stg", "sc"], w=["ru0"])
                                S.add("dve", lambda e, t=t: e.tensor_tensor(
                                    out=ru[:, :, 8:16], in0=xv[:, :, 0:8],
                                    in1=sc[:, t, 24:32].unsqueeze(1).to_broadcast([128, 4, 8]),
                                    op=ALU.mult), r=["stg", "sc"], w=["ru1"])
                                S.add("dve", lambda e, q=q: e.tensor_tensor(
                                    out=qkb[q][:].rearrange("p (m d) -> p m d", d=64)[:, :, 0:16], in0=rt[:], in1=ru[:],
                                    op=ALU.add), r=["rt", "ru0", "ru1"], w=[("qkb", q)])
                        return f

                    def tr_slice(t):
                        def f():
                            q = t % 3
                            bt = rotA.next()
                            for c in range(2):
                                S.add("pe", lambda e, bt=bt, c=c, q=q: e.transpose(
                                    out=bankbf(bt)[:, c * 128:(c + 1) * 128], in_=qkb[q][:, c * 128:(c + 1) * 128],
                                    identity=identb[:]), r=[("qkb", q), "identb"], w=[("ps", bt)])
                            S.add("act", lambda e, bt=bt: e.copy(
                                out=qk_h[:, :, t * 128:(t + 1) * 128],
                                in_=bankbf(bt)[:, 0:256].rearrange("p (a c) -> p a c", c=128)),
                                r=[("ps", bt)], w=qk_w(par, t))
                        return f

                    for t in range(NT):
                        b = pbanks[t % len(pbanks)]
                        for kcs in ((0, 1), (2, 3), (4, 5), (6, 7)):
                            out.append(mm_slice(t, b, kcs))
                        if t >= 2:
                            out.append(tr_slice(t - 2))
                    out.append(tr_slice(NT - 2))
                    out.append(tr_slice(NT - 1))
                    return out

                def drip_late(n):
                    for _ in range(n):
                        if late_ops:
                            a_, k_ = late_ops.pop(0)
                            S.add(*a_, **k_)

                for f in proj_slices(0, [3, 4, 5, 6, 7]):
                    f()
                pending = []
                for h in range(NH):
                    if h + 2 < NH:
                        qslots[h + 2] = load_qkv(h + 2)
                    par = h % 2
                    qk_h = qk2[par]
                    v_h = v2[par]
                    pending = proj_slices(h + 1, [3]) if h + 1 < NH else []
                    n_iter = 40
                    it_ctr = [0]
                    per_iter = -(-len(pending) // (n_iter - 2)) if pending else 0

                    ck("S1proj")
                    for G in range(4):
                        nkb = 4 * G + 4

                        def scores(kb, G=G, par=par, qk_h=qk_h):
                            c0 = max(0, kb - 4 * G) * 128
                            p = ptrot.next()
                            for m in range(2):
                                b = rotA.next()
                                S.add("pe", lambda e, b=b, m=m, kb=kb, c0=c0: e.matmul(
                                    banks[b][:, c0:512],
                                    lhsT=qk_h[m * 64:(m + 1) * 64, 1, kb * 128:(kb + 1) * 128],
                                    rhs=qk_h[m * 64:(m + 1) * 64, 0, G * 512 + c0:(G + 1) * 512],
                                    start=True, stop=True),
                                    r=[("k", par, kb)] + [("q", par, 4 * G + i) for i in range(c0 // 128, 4)], w=[("ps", b)])
                                S.add("act", lambda e, b=b, m=m, p=p, c0=c0: e.activation(
                                    out=pt[p][:, m, c0:512], in_=banks[b][:, c0:512], func=AF.Exp, scale=0.125),
                                    r=[("ps", b)], w=[("pt", p, m)])
                            if kb >= 4 * G:
                                S.add("pool", lambda e, p=p, c0=c0: e.memset(pt[p][64:128, :, c0:c0 + 64], 0.0),
                                      w=[("pt", p, 0), ("pt", p, 1)])
                            return p, c0

                        def pv(kb, p, c0, G=G, par=par, v_h=v_h):
                            for j in range(c0 // 128, 4):
                                qi = 4 * G + j
                                bank = 4 + j
                                for m in range(2):
                                    off = m * 132
                                    S.add("pe", lambda e, bank=bank, off=off, p=p, m=m, j=j, kb=kb, qi=qi: e.matmul(
                                        banks[bank][:, off:off + 129], lhsT=pt[p][:, m, j * 128:(j + 1) * 128],
                                        rhs=v_h[:, kb, 0:129], start=(kb == 0 and m == 0), stop=(kb == qi),
                                        skip_group_check=True),
                                        r=[("pt", p, m), ("v", par, kb), "vones"], w=[("ps", bank)])
                                if kb == qi:
                                    S.add("act", lambda e, bank=bank, j=j: e.copy(
                                        out=acc_sb[:, j, :, :],
                                        in_=banks[bank][:, 0:264].rearrange("p (a c) -> p a c", c=132)),
                                        r=[("ps", bank)], w=[("accsb", j)])

                        prev = scores(0)
                        for kb in range(1, nkb):
                            cur = scores(kb)
                            pv(kb - 1, *prev)
                            prev = cur
                            drip_late(6)
                            it_ctr[0] += 1
                            for _ in range(2 + (1 if it_ctr[0] % 4 == 0 else 0)):
                                if pending:
                                    pending.pop(0)()
                            if kb == min(nkb - 1, 6):
                                drip_late(1000)
                                flush_deferred()
                        pv(nkb - 1, *prev)

                        def finalize(h=h, G=G, add=None):
                            add_late = add or S.add
                            add = S.add
                            AS = [("accsb", j) for j in range(4)]
                            add("dve", lambda e: e.reciprocal(
                                out=rr[:, 0:8].rearrange("p (j m) -> p j m", m=2).unsqueeze(3),
                                in_=acc_sb[:, :, :, 128:129]), r=AS, w=["rr"])
                            add("dve", lambda e: e.tensor_tensor(
                                out=rr[:, 0:8].rearrange("p (j m) -> p j m", m=2)[:, :, 1:2],
                                in0=rr[:, 0:8].rearrange("p (j m) -> p j m", m=2)[:, :, 1:2],
                                in1=nlam[:, 0:1].unsqueeze(2).to_broadcast([128, 4, 1]), op=ALU.mult),
                                r=["rr", "nlam"], w=["rr"])
                            for j in range(4):
                                add("dve", lambda e, j=j: e.tensor_scalar(
                                    out=o_sb[:, j, :], in0=acc_sb[:, j, 0, 0:128], scalar1=rr[:, 2 * j:2 * j + 1], scalar2=None,
                                    op0=ALU.mult), r=[("accsb", j), "rr"], w=[("o", j)])
                                add("dve", lambda e, j=j: e.scalar_tensor_tensor(
                                    out=o_sb[:, j, :], in0=acc_sb[:, j, 1, 0:128], scalar=rr[:, 2 * j + 1:2 * j + 2],
                                    in1=o_sb[:, j, :], op0=ALU.mult, op1=ALU.add),
                                    r=[("accsb", j), "rr", ("o", j)], w=[("o", j)])
                                add_late("dve", lambda e, j=j: e.scalar_tensor_tensor(
                                    out=osq[:, j, :], in0=o_sb[:, j, :], scalar=1.0, in1=o_sb[:, j, :],
                                    op0=ALU.mult, op1=ALU.mult, accum_out=st[:, 1, j:j + 1]),
                                    r=[("o", j)], w=[("junk", j), ("st1", j)])
                            rsqrt_chain(st[:, 1, 0:4], rstd[:, 1, 0:4], 4, 1.0 / 128, SUBLN_EPS,
                                        [("st1", j) for j in range(4)], ["rstd1"], add=add_late)
                            for j in range(4):
                                add_late("dve", lambda e, j=j: e.scalar_tensor_tensor(
                                    out=atok[:, j, :], in0=o_sb[:, j, :], scalar=rstd[:, 1, j:j + 1], in1=subg[:],
                                    op0=ALU.mult, op1=ALU.mult), r=[("o", j), "rstd1", "subg"], w=[("atok", j)])

                            def fin_pe(h=h, G=G):
                                bt = rotA.next()
                                for j in range(4):
                                    S.add("pe", lambda e, bt=bt, j=j: e.transpose(
                                        out=bankbf(bt)[:, j * 128:(j + 1) * 128], in_=atok[:, j, :], identity=identb[:]),
                                        r=[("atok", j), "identb"], w=[("ps", bt)])
                                wk = []
                                for j in range(4):
                                    wk += at_w(h, 4 * G + j)
                                S.add("act", lambda e, bt=bt: e.copy(
                                    out=attnT[:, h, G * 512:(G + 1) * 512], in_=bankbf(bt)[:, 0:512]),
                                    r=[("ps", bt)], w=wk)
                            deferred.append(fin_pe)
                        finalize(add=lambda *a, **k: late_ops.append((a, k)))
                        if G == 3:
                            while pending:
                                pending.pop(0)()
                    ck("S1attn")

                drip_late(1000)
                flush_deferred()
                ck("S1")
                def load_ug(g):
                    slot = wsrot.next()
                    S.add("pool", lambda e, slot=slot, g=g: e.dma_start(
                        out=ws384[slot], in_=wug_d[g].rearrange("(k p) c -> p k c", p=128)),
                        w=[("ws", slot)], chan=("ws", slot))
                    return slot

                ug_pref = [load_ug(0), load_ug(1)]
                for t in range(NT):
                    bs = (rot.next(), rot.next())
                    vb = 2 * (t % 2)
                    for half in range(2):
                        b = bs[half]
                        for kc in range(8):
                            S.add("pe", lambda e, b=b, kc=kc, t=t, half=half: e.matmul(
                                banks[b][:], lhsT=hT[:, kc, t * 128:(t + 1) * 128],
                                rhs=wbufv[:, kc, half * 512:(half + 1) * 512], start=(kc == 0), stop=(kc == 7)),
                                r=[("hT", t), "wbuf"], w=[("ps", b)])
                        S.add("act", lambda e, b=b, half=half, vb=vb: e.activation(out=scr[:, vb + half, :], in_=banks[b][:],
                                                                                   func=AF.Gelu),
                              r=[("ps", b)], w=[("scr", vb + half)])
                    vg = scr[:, vb:vb + 2, :].rearrange("p a c -> p (a c)")
                    vk = [("scr", vb), ("scr", vb + 1)]
                    S.add("dve", lambda e, t=t, vg=vg: e.scalar_tensor_tensor(
                        out=junk[:], in0=vg, scalar=1.0, in1=vg, op0=ALU.mult, op1=ALU.mult,
                        accum_out=st[:, 2, t:t + 1]), r=vk, w=[("junk", i) for i in range(8)] + [("st2", t)])
                    rsqrt_chain(st[:, 2, t:t + 1], rstd[:, 2, t:t + 1], 1, 1.0 / D, EPS, [("st2", t)], [("rstd2", t)])
                    S.add("dve", lambda e, t=t, vg=vg: e.tensor_scalar(out=vn[:, t, :], in0=vg, scalar1=rstd[:, 2, t:t + 1],
                                                                       scalar2=None, op0=ALU.mult),
                          r=vk + [("rstd2", t)], w=vn_w(t))

                ck("C1")
                S.add("pool", lambda e: e.dma_start(out=wbufv, in_=wout_d.rearrange("(k p) c -> p k c", p=128)),
                      w=["wbuf"] + X2A, chan="wbuf")

                slots = ug_pref
                for g in range(NH):
                    slot = slots[g]
                    if g + 2 < NH:
                        slots.append(load_ug(g + 2))
                    for tg in range(4):
                        bu, ba, bb, bm = rot.next(), rot.next(), rot.next(), rot.next()
                        for jx, b in enumerate((bu, ba, bb)):
                            for kc in range(8):
                                S.add("pe", lambda e, b=b, kc=kc, jx=jx, tg=tg, slot=slot: e.matmul(
                                    banks[b][:], lhsT=ws384[slot][:, kc, jx * 128:(jx + 1) * 128],
                                    rhs=hT[:, kc, tg * 512:(tg + 1) * 512], start=(kc == 0), stop=(kc == 7)),
                                    r=[("ws", slot)] + [("hT", 4 * tg + i) for i in range(4)], w=[("ps", b)])
                        for i in range(4):
                            S.add("pe", lambda e, bm=bm, i=i, tg=tg, g=g: e.matmul(
                                banks[bm][:, i * 128:(i + 1) * 128], lhsT=vn[:, 4 * tg + i, g * 128:(g + 1) * 128],
                                rhs=wT[:, g, :], start=True, stop=True),
                                r=[("vn", 4 * tg + i), "wT"], w=[("ps", bm)])
                        S.add("act", lambda e, bu=bu: e.activation(out=scr[:, 0, :], in_=banks[bu][:], func=AF.Gelu),
                              r=[("ps", bu)], w=[("scr", 0)])
                        S.add("act", lambda e, ba=ba, g=g: e.activation(out=scr[:, 1, :], in_=banks[ba][:], func=AF.Tanh,
                                                                        bias=hb[:, g:g + 1], scale=0.5),
                              r=[("ps", ba), "hb"], w=[("scr", 1)])
                        S.add("act", lambda e, bb=bb, g=g: e.activation(out=scr[:, 2, :], in_=banks[bb][:], func=AF.Tanh,
                                                                        bias=hb[:, 8 + g:9 + g], scale=0.5),
                              r=[("ps", bb), "hb"], w=[("scr", 2)])
                        S.add("dve", lambda e, bm=bm, g=g: e.scalar_tensor_tensor(
                            out=scr[:, 3, :].rearrange("p (a c) -> p a c", c=128),
                            in0=banks[bm][:].rearrange("p (a c) -> p a c", c=128), scalar=gnh[:, g:g + 1],
                            in1=b_half[:, g * 128:(g + 1) * 128].unsqueeze(1).to_broadcast([128, 4, 128]),
                            op0=ALU.mult, op1=ALU.add), r=[("ps", bm), "gnh", "b_half"], w=[("scr", 3)])
                        S.add("pool", lambda e: e.tensor_tensor(out=scr[:, 3, :], in0=scr[:, 3, :], in1=scr[:, 0, :],
                                                                op=ALU.mult), r=[("scr", 3), ("scr", 0)], w=[("scr", 3)])
                        S.add("dve", lambda e: e.scalar_tensor_tensor(out=scr[:, 3, :], in0=scr[:, 2, :], scalar=1.0,
                                                                      in1=scr[:, 3, :], op0=ALU.add, op1=ALU.mult),
                              r=[("scr", 2), ("scr", 3)], w=[("scr", 3)])
                        atk = [("at", g, 4 * tg + i) for i in range(4)]
                        atw = []
                        for i in range(4):
                            atw += at_w(g, 4 * tg + i)
                        S.add("dve", lambda e, g=g, tg=tg: e.scalar_tensor_tensor(
                            out=scr[:, 4, :], in0=scr[:, 1, :], scalar=1.0, in1=attnT[:, g, tg * 512:(tg + 1) * 512],
                            op0=ALU.add, op1=ALU.mult), r=[("scr", 1)] + atk, w=[("scr", 4)])
                        S.add("pool", lambda e, g=g, tg=tg: e.tensor_tensor(
                            out=attnT[:, g, tg * 512:(tg + 1) * 512], in0=scr[:, 3, :], in1=scr[:, 4, :], op=ALU.add),
                            r=[("scr", 3), ("scr", 4)], w=atw)

                ck("C2")
                def load_ffi(j):
                    slot = wsrot.next()
                    S.add("pool", lambda e, slot=slot, j=j: e.dma_start(
                        out=wsffi[slot], in_=wffi_d[j].rearrange("(k p) c -> p k c", p=128)),
                        w=[("ws", slot)], chan=("ws", slot))
                    return slot

                ffi_pref = [load_ffi(0), load_ffi(1)]

                def c3_ld(t):
                    sl = t % 4
                    S.add("sp", lambda e, sl=sl, t=t, s=s: e.dma_start(out=XS_AP[sl], in_=x_d[s, t * 128:(t + 1) * 128, :]),
                          w=XS_K[sl], chan=("xs", sl))

                def c3_mm(t):
                    sl = t % 4
                    bs = (rot.next(), rot.next())
                    for half in range(2):
                        b = bs[half]
                        for kc in range(8):
                            S.add("pe", lambda e, b=b, kc=kc, t=t, half=half: e.matmul(
                                banks[b][:], lhsT=attnT[:, kc, t * 128:(t + 1) * 128],
                                rhs=wbufv[:, kc, half * 512:(half + 1) * 512], start=(kc == 0), stop=(kc == 7)),
                                r=[("at", kc, t), "wbuf"], w=[("ps", b)])
                        hk = xs_hk(sl, half)
                        S.add("dve", lambda e, b=b, sl=sl, half=half: e.tensor_tensor(
                            out=xs_half(sl, half), in0=banks[b][:], in1=xs_half(sl, half), op=ALU.add),
                            r=[("ps", b)] + hk, w=hk)
                    S.add("sp", lambda e, sl=sl, t=t, s=s: e.dma_start(out=x1_d[s, t * 128:(t + 1) * 128, :], in_=XS_AP[sl]),
                          r=XS_K[sl], w=[("x1", s, t)], chan=("xst", sl))
                    n3 = t % 3
                    S.add("dve", lambda e, sl=sl, t=t: e.scalar_tensor_tensor(
                        out=junk[:], in0=XS_AP[sl], scalar=1.0, in1=XS_AP[sl], op0=ALU.mult, op1=ALU.mult,
                        accum_out=st[:, 3, t:t + 1]), r=XS_K[sl], w=[("junk", i) for i in range(8)] + [("st3", t)])
                    rsqrt_act(st[:, 3, t:t + 1], rstd[:, 3, t:t + 1], 1, 1.0 / D, EPS, [("st3", t)], [("rstd3", t)])
                    S.add("act", lambda e, sl=sl, t=t, n3=n3: e.activation(out=xn3[n3][:], in_=XS_AP[sl], func=AF.Copy,
                                                                           scale=rstd[:, 3, t:t + 1]),
                          r=XS_K[sl] + [("rstd3", t)], w=[("xn", n3)])

                def c3_tr(t):
                    n3 = t % 3
                    b = rot.next()
                    for kc in range(8):
                        S.add("pe", lambda e, b=b, kc=kc, n3=n3: e.transpose(out=bankbf(b)[:, kc * 128:(kc + 1) * 128],
                                                                            in_=xn3[n3][:, kc * 128:(kc + 1) * 128],
                                                                            identity=identb[:]),
                              r=[("xn", n3), "identb"], w=[("ps", b)])
                    S.add("dve", lambda e, b=b, t=t: e.tensor_tensor(
                        out=hT[:, :, t * 128:(t + 1) * 128],
                        in0=bankbf(b).rearrange("p (k c) -> p k c", c=128),
                        in1=gffn[:].unsqueeze(2).to_broadcast([128, 8, 128]), op=ALU.mult),
                        r=[("ps", b), "gffn"], w=[("hT", t)])

                for t in range(3):
                    c3_ld(t)
                for t in range(NT):
                    if t + 3 < NT:
                        c3_ld(t + 3)
                    c3_mm(t)
                    if t >= 2:
                        c3_tr(t - 2)
                c3_tr(NT - 2)
                c3_tr(NT - 1)

                ck("C3")
                for hf in range(2):
                    slots = list(ffi_pref)
                    for j in range(NJ):
                        slot = slots[j]
                        if j + 2 < NJ:
                            slots.append(load_ffi(j + 2))
                        for tg2 in range(2):
                            tok0 = hf * 1024 + tg2 * 512
                            ba, bb = rot.next(), rot.next()
                            for c, b in ((0, ba), (1, bb)):
                                for kc in range(8):
                                    S.add("pe", lambda e, b=b, kc=kc, c=c, tok0=tok0, slot=slot: e.matmul(
                                        banks[b][:], lhsT=wsffi[slot][:, kc, c * 128:(c + 1) * 128],
                                        rhs=hT[:, kc, tok0:tok0 + 512], start=(kc == 0), stop=(kc == 7)),
                                        r=[("ws", slot)] + [("hT", tok0 // 128 + i) for i in range(4)], w=[("ps", b)])
                            q = (2 * j + tg2) % 2
                            S.add("act", lambda e, ba=ba, q=q: e.activation(out=scr[:, q, :], in_=banks[ba][:], func=AF.Silu),
                                  r=[("ps", ba)], w=[("scr", q)])
                            S.add("dve", lambda e, bb=bb, q=q, j=j, tg2=tg2: e.tensor_tensor(
                                out=fT[:, j, tg2 * 512:(tg2 + 1) * 512], in0=scr[:, q, :], in1=banks[bb][:], op=ALU.mult),
                                r=[("scr", q), ("ps", bb)], w=fT_w(j, tg2))

                    ck("FFNin")
                    S.add("dve", lambda e: e.memset(fence[:], 0.0), w=["wbuf", "fence"])
                    for rnd in range(2):
                        for tl in range(8):
                            t = hf * 8 + tl
                            S.add("sp", lambda e, tl=tl, t=t, s=s, rnd=rnd: e.dma_start(
                                out=xslot(tl), in_=x1_d[s, t * 128:(t + 1) * 128, rnd * 512:(rnd + 1) * 512]),
                                r=[("x1", s, t)], w=xslot_keys(tl), chan=("xsl", tl))
                        for jj in range(0, NJ, 6):
                            nj = min(6, NJ - jj)
                            slot = wsrot.next()
                            S.add("pool", lambda e, slot=slot, jj=jj, nj=nj, rnd=rnd: e.dma_start(
                                out=wsffo[slot][:, 0:nj, :],
                                in_=wffo_d[jj:jj + nj].rearrange("j p c -> p j c")[:, :, rnd * 512:(rnd + 1) * 512]),
                                w=[("ws", slot)], chan=("ws", slot))
                            for j in range(jj, jj + nj):
                                for tl in range(8):
                                    S.add("pe", lambda e, tl=tl, j=j, jj=jj, slot=slot: e.matmul(
                                        banks[tl][:], lhsT=fT[:, j, tl * 128:(tl + 1) * 128], rhs=wsffo[slot][:, j - jj, :],
                                        start=(j == 0), stop=(j == NJ - 1)),
                                        r=[("fT", j, tl // 4), ("ws", slot)], w=[("ps", tl)])
                        if rnd == 1 and hf == 0:
                            ffi_pref = [load_ffi(0), load_ffi(1)]
                        for tl in range(8):
                            if rnd == 0:
                                S.add("dve", lambda e, tl=tl: e.tensor_tensor(
                                    out=x2a[:, tl, :], in0=banks[tl][:], in1=xslot(tl), op=ALU.add),
                                    r=[("ps", tl), "fence"] + xslot_keys(tl), w=[("x2a", tl)])
                            else:
                                S.add("dve", lambda e, tl=tl: e.tensor_tensor(
                                    out=xslot(tl), in0=banks[tl][:], in1=xslot(tl), op=ALU.add),
                                    r=[("ps", tl)], w=xslot_keys(tl))
                        for tl in range(8):
                            jk = [("junk", 4 * (tl % 2) + i) for i in range(4)]
                            if rnd == 0:
                                S.add("act", lambda e, tl=tl: e.activation(
                                    out=junk[:, (tl % 2) * 512:(tl % 2) * 512 + 512], in_=x2a[:, tl, :], func=AF.Square,
                                    accum_out=st[:, 4, tl:tl + 1]), r=[("x2a", tl)], w=jk + [("st4", tl)])
                            else:
                                S.add("act", lambda e, tl=tl: e.activation(
                                    out=junk[:, (tl % 2) * 512:(tl % 2) * 512 + 512], in_=xslot(tl), func=AF.Square,
                                    accum_out=st[:, 5, tl:tl + 1]), r=xslot_keys(tl), w=jk + [("st5", tl)])
                        if rnd == 1:
                            S.add("dve", lambda e: e.tensor_tensor(out=st[:, 6, 0:8], in0=st[:, 4, 0:8], in1=st[:, 5, 0:8],
                                                                   op=ALU.add),
                                  r=[("st4", tl) for tl in range(8)] + [("st5", tl) for tl in range(8)], w=["st6"])
                            rsqrt_chain(st[:, 6, 0:8], rstd[:, 6, 0:8], 8, 1.0 / D, EPS, ["st6"], ["rstd6"])
                            for tl in range(8):
                                t = hf * 8 + tl
                                S.add("dve", lambda e, tl=tl: e.scalar_tensor_tensor(
                                    out=x2a[:, tl, :], in0=x2a[:, tl, :], scalar=rstd[:, 6, tl:tl + 1], in1=gfin[:, 0:512],
                                    op0=ALU.mult, op1=ALU.mult), r=[("x2a", tl), "rstd6", "gfin"], w=[("x2a", tl)])
                                S.add("sp", lambda e, tl=tl, t=t, s=s: e.dma_start(
                                    out=out_d[s, t * 128:(t + 1) * 128, 0:512], in_=x2a[:, tl, :]),
                                    r=[("x2a", tl)], w=[("out", s, t, 0)], chan=("o1", tl))
                                S.add("dve", lambda e, tl=tl: e.scalar_tensor_tensor(
                                    out=xslot(tl), in0=xslot(tl), scalar=rstd[:, 6, tl:tl + 1], in1=gfin[:, 512:1024],
                                    op0=ALU.mult, op1=ALU.mult), r=xslot_keys(tl) + ["rstd6", "gfin"], w=xslot_keys(tl))
                                S.add("sp", lambda e, tl=tl, t=t, s=s: e.dma_start(
                                    out=out_d[s, t * 128:(t + 1) * 128, 512:1024], in_=xslot(tl)),
                                    r=xslot_keys(tl), w=[("out", s, t, 1)], chan=("o2", tl))
                                out_keys.append(("out", s, t, 0))
                                out_keys.append(("out", s, t, 1))

        except _Stop:
            pass
        if stop is not None:
            out_keys = list(S.lastw.keys())
        S.add("sp", lambda e: e.nop(), r=out_keys)
        S.emit(nc)
    return nc


_CACHE = {}


def _prep_shared(inp):
    f = np.float32
    w_in = np.asarray(inp["w_in"], f)[0]
    qw, kw, vw = w_in[:, 0:1024], w_in[:, 1024:2048], w_in[:, 2048:3072]
    uw, gvw = w_in[:, 3072:4096], w_in[:, 4096:5120]
    gaw, gbw = w_in[:, 5120:6144], w_in[:, 6144:7168]
    sl = lambda a, h: a[:, h * 128:(h + 1) * 128]
    w_qkv = np.ascontiguousarray(np.stack(
        [np.concatenate([sl(qw, h), sl(kw, h), sl(vw, h)], axis=1) for h in range(NH)]))
    w_ug = np.ascontiguousarray(np.stack(
        [np.concatenate([sl(uw, g), sl(gaw, g), sl(gbw, g)], axis=1) for g in range(NH)]))
    wfi = np.asarray(inp["w_ffn_in"], f)[0]
    w_ffi = np.ascontiguousarray(np.stack(
        [np.concatenate([wfi[:, j * 128:(j + 1) * 128], wfi[:, DFF + j * 128:DFF + (j + 1) * 128]], axis=1)
         for j in range(NJ)]))
    w_ffo = np.ascontiguousarray(np.asarray(inp["w_ffn_out"], f)[0].reshape(NJ, 128, D))
    colT = lambda v: np.ascontiguousarray(np.asarray(v, f).reshape(8, 128).T)
    gate_b = np.asarray(inp["gate_b"], f)[0]
    gate_bl = np.ascontiguousarray(gate_b.reshape(2, 8, 128).transpose(2, 0, 1).reshape(128, 16))
    gm_wT = np.ascontiguousarray(np.asarray(inp["gm_ws"], f)[0].transpose(2, 0, 1).reshape(128, 1024))
    inv_freq = (500000.0 ** (-np.arange(0, 16, 2, dtype=np.float32) / 16)).astype(np.float32)
    invf = (inv_freq.astype(np.float64) / (2 * math.pi)).astype(f).reshape(1, 8)
    return dict(
        w_qkv=w_qkv, w_gv=np.ascontiguousarray(gvw), w_ug=w_ug,
        w_out=np.ascontiguousarray(np.asarray(inp["w_out"], f)[0]),
        w_ffi=w_ffi, w_ffo=w_ffo,
        g_mix=colT(inp["norm_mix_g"][0]), g_ffn=colT(inp["norm_ffn_g"][0]), g_gm=colT(inp["gm_norm_g"][0]),
        gate_bl=gate_bl,
        lambdas=np.ascontiguousarray(np.asarray(inp["lambdas"], f)[0].reshape(1, 256)),
        subln_g=np.ascontiguousarray(np.asarray(inp["subln_g"], f)[0].reshape(1, 128)),
        gm_wT=gm_wT,
        gm_b=np.ascontiguousarray(np.asarray(inp["gm_bs"], f)[0].reshape(1, 1024)),
        g_fin=np.ascontiguousarray(np.asarray(inp["norm_final_g"], f).reshape(1, 1024)),
        ident=np.eye(128, dtype=f), invf=invf,
    )


def kernel(**inputs):
    x = np.asarray(inputs["x"], np.float32)
    pos = np.asarray(inputs["positions"], np.int32)
    shared = _prep_shared(inputs)
    if "nc" not in _CACHE:
        _CACHE["nc"] = build_program()
    nc = _CACHE["nc"]
    in_maps = []
    for c in range(N_CORES):
        m = dict(shared)
        m["x"] = np.ascontiguousarray(x[c * NSEQ:(c + 1) * NSEQ])
        m["pos"] = np.ascontiguousarray(
            pos[c * NSEQ:(c + 1) * NSEQ].reshape(NSEQ, NT, 128).transpose(0, 2, 1))
        in_maps.append(m)
    res = run_bass_kernel_spmd(nc, in_maps, core_ids=list(range(N_CORES)))
    out = np.concatenate([np.asarray(r["out"]) for r in res.results], axis=0)
    return out.astype(np.float32)
```
